# Optimizing a Trainium2 kernel written in Bass

```python
import jax, jax.numpy as jnp
from jax import lax
import numpy as np

D_MODEL = 1024
BATCH = 4
SEQ = 4096
DEPTH = 4

CHUNK = 64
N_MIXERS = 3
D_FF = 4 * D_MODEL
CONV_WIDTH = 4
RMS_EPS = 1e-6

LRU_WIDTH = D_MODEL
LRU_BLOCK_DIM = 256
LRU_BLOCKS = LRU_WIDTH // LRU_BLOCK_DIM
LRU_C = 8.0

RET_DK = 256
RET_HEADS = D_MODEL // RET_DK
RET_DV = 2 * RET_DK
RET_QK_WIDTH = RET_HEADS * RET_DK
RET_V_WIDTH = RET_HEADS * RET_DV
RET_IN_WIDTH = 2 * RET_QK_WIDTH + 2 * RET_V_WIDTH
ROPE_BASE = 10000.0

GDN_DK = 128
GDN_DV = 128
GDN_QK_HEADS = D_MODEL // GDN_DK
GDN_V_HEADS = 2 * GDN_QK_HEADS
GDN_QK_WIDTH = GDN_QK_HEADS * GDN_DK
GDN_V_WIDTH = GDN_V_HEADS * GDN_DV
GDN_CONV_CH = 2 * GDN_QK_WIDTH + GDN_V_WIDTH
GDN_IN_WIDTH = GDN_CONV_CH + GDN_V_WIDTH + 2 * GDN_V_HEADS

kernel_name = "hybrid_rglru_retention_gdn_trunk"


def _n_layers_of(kind):
    return len(range(kind, DEPTH, N_MIXERS))


def rms_norm(x, g=None, eps=RMS_EPS):
    xf = x.astype(jnp.float32)
    y = xf * lax.rsqrt(jnp.mean(xf * xf, axis=-1, keepdims=True) + eps)
    if g is not None:
        y = y * g.astype(jnp.float32)
    return y.astype(x.dtype)


def l2_norm(x, eps=1e-6):
    xf = x.astype(jnp.float32)
    return xf * lax.rsqrt(jnp.sum(xf * xf, axis=-1, keepdims=True) + eps)


def causal_depthwise_conv(x, w):
    c = x.shape[-1]
    return lax.conv_general_dilated(
        x, w[:, None, :].astype(x.dtype), window_strides=(1,),
        padding=[(CONV_WIDTH - 1, 0)], dimension_numbers=('NWC', 'WIO', 'NWC'),
        feature_group_count=c)


def sq_relu_mlp(h, w_up, w_down):
    return jnp.square(jax.nn.relu(h @ w_up)) @ w_down


def _linear_combine(c1, c2):
    a1, b1 = c1
    a2, b2 = c2
    return a1 * a2, a2 * b1 + b2


def rglru_mixer(h, w_in, conv_w, conv_b, wa, ba, wx, bx, lam, w_out):
    b_, s_, _ = h.shape
    gate, xr = jnp.split(h @ w_in, 2, axis=-1)
    xr = causal_depthwise_conv(xr, conv_w) + conv_b
    xb = xr.reshape(b_, s_, LRU_BLOCKS, LRU_BLOCK_DIM)
    r = jax.nn.sigmoid(jnp.einsum('bsgi,gij->bsgj', xb, wa) + ba).reshape(b_, s_, LRU_WIDTH)
    i = jax.nn.sigmoid(jnp.einsum('bsgi,gij->bsgj', xb, wx) + bx).reshape(b_, s_, LRU_WIDTH)
    log_a = -LRU_C * r.astype(jnp.float32) * jax.nn.softplus(-lam.astype(jnp.float32))
    a = jnp.exp(log_a)
    u = jnp.sqrt(-jnp.expm1(2.0 * log_a)) * (i * xr).astype(jnp.float32)
    _, hs = lax.associative_scan(_linear_combine, (a, u), axis=1)
    y = jax.nn.gelu(gate) * hs.astype(gate.dtype)
    return y @ w_out


def rope_tables(s_):
    pos = jnp.arange(s_, dtype=jnp.float32)
    inv_freq = ROPE_BASE ** (-jnp.arange(0, RET_DK, 2, dtype=jnp.float32) / RET_DK)
    ang = pos[:, None] * inv_freq[None, :]
    return jnp.cos(ang), jnp.sin(ang)


def apply_rope(x, cos, sin):
    x1, x2 = jnp.split(x, 2, axis=-1)
    c = cos[None, :, None, :]
    s = sin[None, :, None, :]
    return jnp.concatenate([x1 * c - x2 * s, x2 * c + x1 * s], axis=-1)


def retention_chunkwise(q, k, v):
    b_, s_, nh, dk = q.shape
    dv = v.shape[-1]
    nc = s_ // CHUNK
    log_gamma = jnp.log1p(-jnp.exp2(-5.0 - jnp.arange(nh, dtype=jnp.float32)))
    pos = jnp.arange(CHUNK, dtype=jnp.float32)
    diff = pos[:, None] - pos[None, :]
    dmask = jnp.where(diff >= 0, jnp.exp(log_gamma[:, None, None] * jnp.maximum(diff, 0.0)), 0.0)
    inter_dec = jnp.exp(log_gamma[:, None] * (pos + 1.0)).T[None, :, :, None]
    k_dec = jnp.exp(log_gamma[:, None] * (CHUNK - 1.0 - pos)).T[None, :, :, None]
    chunk_dec = jnp.exp(log_gamma * CHUNK)[None, :, None, None]

    def to_chunks(t):
        return t.reshape(b_, nc, CHUNK, nh, t.shape[-1]).transpose(1, 0, 2, 3, 4)

    def body(state, xs):
        qc, kc, vc = xs
        scores = jnp.einsum('bihd,bjhd->bhij', qc, kc) * dmask
        o = (jnp.einsum('bhij,bjhe->bihe', scores, vc)
             + jnp.einsum('bihd,bhde->bihe', qc, state) * inter_dec)
        state = state * chunk_dec + jnp.einsum('bjhd,bjhe->bhde', kc * k_dec, vc)
        return state, o

    s0 = jnp.zeros((b_, nh, dk, dv), jnp.float32)
    _, o = lax.scan(body, s0, (to_chunks(q), to_chunks(k), to_chunks(v)))
    return o.transpose(1, 0, 2, 3, 4).reshape(b_, s_, nh, dv)


def retention_mixer(h, w_in, w_out, cos, sin):
    b_, s_, _ = h.shape
    q, k, v, g = jnp.split(h @ w_in, [RET_QK_WIDTH, 2 * RET_QK_WIDTH,
                                      2 * RET_QK_WIDTH + RET_V_WIDTH], axis=-1)
    q = apply_rope(q.reshape(b_, s_, RET_HEADS, RET_DK).astype(jnp.float32), cos, sin)
    k = apply_rope(k.reshape(b_, s_, RET_HEADS, RET_DK).astype(jnp.float32), cos, sin) * (RET_DK ** -0.5)
    v = v.reshape(b_, s_, RET_HEADS, RET_DV).astype(jnp.float32)
    o = rms_norm(retention_chunkwise(q, k, v))
    o = o.reshape(b_, s_, RET_V_WIDTH).astype(h.dtype)
    return (jax.nn.silu(g) * o) @ w_out


def _unit_lower_solve(a, b):
    return lax.linalg.triangular_solve(a, b, left_side=True, lower=True, unit_diagonal=True)


def gated_delta_chunkwise(q, k, v, g, beta):
    b_, s_, nh, dk = q.shape
    dv = v.shape[-1]
    nc = s_ // CHUNK

    def to_chunks(t):
        return t.reshape(b_, nc, CHUNK, nh, t.shape[-1]).transpose(0, 3, 1, 2, 4)

    qc, kc, vc = to_chunks(q), to_chunks(k), to_chunks(v)
    gc = g.reshape(b_, nc, CHUNK, nh).transpose(0, 3, 1, 2)
    bc = beta.reshape(b_, nc, CHUNK, nh).transpose(0, 3, 1, 2)
    G = jnp.cumsum(gc, axis=-1)
    causal = jnp.tril(jnp.ones((CHUNK, CHUNK), dtype=bool))
    strict = jnp.tril(jnp.ones((CHUNK, CHUNK), dtype=bool), k=-1)
    decay = jnp.exp(jnp.where(causal, G[..., :, None] - G[..., None, :], -jnp.inf))
    k_beta = kc * bc[..., None]
    v_beta = vc * bc[..., None]
    a_mat = jnp.where(strict, jnp.einsum('bhncd,bhnjd->bhncj', k_beta, kc) * decay, 0.0)
    u = _unit_lower_solve(a_mat, v_beta)
    w = _unit_lower_solve(a_mat, k_beta * jnp.exp(G)[..., None])
    attn = jnp.einsum('bhncd,bhnjd->bhncj', qc, kc) * decay
    q_dec = qc * jnp.exp(G)[..., None]
    k_dec = kc * jnp.exp(G[..., -1:] - G)[..., None]
    g_tot = jnp.exp(G[..., -1])

    def body(state, xs):
        q_i, w_i, u_i, attn_i, k_i, gt_i = xs
        v_new = u_i - jnp.einsum('bhcd,bhde->bhce', w_i, state)
        o = (jnp.einsum('bhcd,bhde->bhce', q_i, state)
             + jnp.einsum('bhcj,bhje->bhce', attn_i, v_new))
        state = state * gt_i[..., None, None] + jnp.einsum('bhcd,bhce->bhde', k_i, v_new)
        return state, o

    xs = tuple(jnp.moveaxis(t, 2, 0) for t in (q_dec, w, u, attn, k_dec, g_tot))
    s0 = jnp.zeros((b_, nh, dk, dv), jnp.float32)
    _, o = lax.scan(body, s0, xs)
    return o.transpose(1, 0, 3, 2, 4).reshape(b_, s_, nh, dv)


def gdn_mixer(h, w_in, conv_w, a_log, dt_bias, norm_w, w_out):
    b_, s_, _ = h.shape
    qkv, z, b_logit, a_in = jnp.split(h @ w_in, [GDN_CONV_CH, GDN_CONV_CH + GDN_V_WIDTH,
                                                 GDN_CONV_CH + GDN_V_WIDTH + GDN_V_HEADS], axis=-1)
    qkv = jax.nn.silu(causal_depthwise_conv(qkv, conv_w))
    q, k, v = jnp.split(qkv, [GDN_QK_WIDTH, 2 * GDN_QK_WIDTH], axis=-1)
    rep = GDN_V_HEADS // GDN_QK_HEADS
    q = jnp.repeat(l2_norm(q.reshape(b_, s_, GDN_QK_HEADS, GDN_DK)), rep, axis=2) * (GDN_DK ** -0.5)
    k = jnp.repeat(l2_norm(k.reshape(b_, s_, GDN_QK_HEADS, GDN_DK)), rep, axis=2)
    v = v.reshape(b_, s_, GDN_V_HEADS, GDN_DV).astype(jnp.float32)
    beta = jax.nn.sigmoid(b_logit.astype(jnp.float32))
    g = -jnp.exp(a_log.astype(jnp.float32)) * jax.nn.softplus(a_in.astype(jnp.float32) + dt_bias.astype(jnp.float32))
    o = gated_delta_chunkwise(q, k, v, g, beta)
    o = rms_norm(o, norm_w) * jax.nn.silu(z.reshape(b_, s_, GDN_V_HEADS, GDN_DV).astype(jnp.float32))
    return o.reshape(b_, s_, GDN_V_WIDTH).astype(h.dtype) @ w_out


def setup_inputs(seed: int = 0) -> dict:
    key = jax.random.key(seed)
    ks = iter(jax.random.split(key, 32))
    f32 = jnp.float32
    n_a, n_b, n_c = _n_layers_of(0), _n_layers_of(1), _n_layers_of(2)

    def nrm(shape, scale):
        return jax.random.normal(next(ks), shape, f32) * scale

    def gain(shape):
        return 1.0 + nrm(shape, 0.02)

    x = nrm((BATCH, SEQ, D_MODEL), 1.0)
    mix_norm = gain((DEPTH, D_MODEL))
    mlp_norm = gain((DEPTH, D_MODEL))
    mlp_w_up = nrm((DEPTH, D_MODEL, D_FF), D_MODEL ** -0.5)
    mlp_w_down = nrm((DEPTH, D_FF, D_MODEL), 0.5 * D_FF ** -0.5)

    lru_w_in = nrm((n_a, D_MODEL, 2 * LRU_WIDTH), D_MODEL ** -0.5)
    lru_conv_w = nrm((n_a, CONV_WIDTH, LRU_WIDTH), CONV_WIDTH ** -0.5)
    lru_conv_b = nrm((n_a, LRU_WIDTH), 0.01)
    lru_wa = nrm((n_a, LRU_BLOCKS, LRU_BLOCK_DIM, LRU_BLOCK_DIM), LRU_BLOCK_DIM ** -0.5)
    lru_ba = nrm((n_a, LRU_BLOCKS, LRU_BLOCK_DIM), 0.01)
    lru_wx = nrm((n_a, LRU_BLOCKS, LRU_BLOCK_DIM, LRU_BLOCK_DIM), LRU_BLOCK_DIM ** -0.5)
    lru_bx = nrm((n_a, LRU_BLOCKS, LRU_BLOCK_DIM), 0.01)
    s_rad = jnp.sqrt(jax.random.uniform(next(ks), (n_a, LRU_WIDTH), f32, 0.9 ** 2, 0.999 ** 2))
    lru_lambda = jnp.log(s_rad) - jnp.log1p(-s_rad)
    lru_w_out = nrm((n_a, LRU_WIDTH, D_MODEL), LRU_WIDTH ** -0.5)

    ret_w_in = nrm((n_b, D_MODEL, RET_IN_WIDTH), D_MODEL ** -0.5)
    ret_w_out = nrm((n_b, RET_V_WIDTH, D_MODEL), RET_V_WIDTH ** -0.5)

    gdn_w_in = nrm((n_c, D_MODEL, GDN_IN_WIDTH), D_MODEL ** -0.5)
    gdn_conv_w = nrm((n_c, CONV_WIDTH, GDN_CONV_CH), CONV_WIDTH ** -0.5)
    gdn_a_log = jnp.log(jax.random.uniform(next(ks), (n_c, GDN_V_HEADS), f32, 1.0, 16.0))
    dt = jnp.exp(jax.random.uniform(next(ks), (n_c, GDN_V_HEADS), f32, np.log(0.001), np.log(0.1)))
    gdn_dt_bias = dt + jnp.log(-jnp.expm1(-dt))
    gdn_norm = gain((n_c, GDN_DV))
    gdn_w_out = nrm((n_c, GDN_V_WIDTH, D_MODEL), GDN_V_WIDTH ** -0.5)

    final_norm = gain((D_MODEL,))
    return {
        'x': x, 'mix_norm': mix_norm, 'mlp_norm': mlp_norm,
        'mlp_w_up': mlp_w_up, 'mlp_w_down': mlp_w_down,
        'lru_w_in': lru_w_in, 'lru_conv_w': lru_conv_w, 'lru_conv_b': lru_conv_b,
        'lru_wa': lru_wa, 'lru_ba': lru_ba, 'lru_wx': lru_wx, 'lru_bx': lru_bx,
        'lru_lambda': lru_lambda, 'lru_w_out': lru_w_out,
        'ret_w_in': ret_w_in, 'ret_w_out': ret_w_out,
        'gdn_w_in': gdn_w_in, 'gdn_conv_w': gdn_conv_w, 'gdn_a_log': gdn_a_log,
        'gdn_dt_bias': gdn_dt_bias, 'gdn_norm': gdn_norm, 'gdn_w_out': gdn_w_out,
        'final_norm': final_norm,
    }


def reference(x, mix_norm, mlp_norm, mlp_w_up, mlp_w_down,
              lru_w_in, lru_conv_w, lru_conv_b, lru_wa, lru_ba, lru_wx, lru_bx,
              lru_lambda, lru_w_out,
              ret_w_in, ret_w_out,
              gdn_w_in, gdn_conv_w, gdn_a_log, gdn_dt_bias, gdn_norm, gdn_w_out,
              final_norm):
    _, s_, _ = x.shape
    cos, sin = rope_tables(s_)
    h = x
    for i in range(DEPTH):
        kind, j = i % N_MIXERS, i // N_MIXERS
        u = rms_norm(h, mix_norm[i])
        if kind == 0:
            m = rglru_mixer(u, lru_w_in[j], lru_conv_w[j], lru_conv_b[j], lru_wa[j], lru_ba[j],
                            lru_wx[j], lru_bx[j], lru_lambda[j], lru_w_out[j])
        elif kind == 1:
            m = retention_mixer(u, ret_w_in[j], ret_w_out[j], cos, sin)
        else:
            m = gdn_mixer(u, gdn_w_in[j], gdn_conv_w[j], gdn_a_log[j], gdn_dt_bias[j],
                          gdn_norm[j], gdn_w_out[j])
        h = h + m
        h = h + sq_relu_mlp(rms_norm(h, mlp_norm[i]), mlp_w_up[i], mlp_w_down[i])
    return rms_norm(h, final_norm)
```

```python
import contextlib
import numpy as np
import concourse.bass as bass
import concourse.mybir as mybir
from concourse.bass_utils import run_bass_kernel_spmd

F32 = mybir.dt.float32
BF16 = mybir.dt.bfloat16
AF = mybir.ActivationFunctionType
ALU = mybir.AluOpType
AX = mybir.AxisListType

D = 1024
DFF = 4096
NCORES = 8
RMS_EPS = 1e-6
KINDS = (0, 1, 2, 0)
NSLOT = {0: 7, 1: 16, 2: 16}


class Buf:
    __slots__ = ("last_w", "readers")

    def __init__(self):
        self.last_w = None
        self.readers = []


class Op:
    __slots__ = ("eng", "fn", "deps", "dma", "sig", "waits", "consumed", "pre")

    def __init__(self, eng, fn, dma):
        self.eng = eng
        self.fn = fn
        self.dma = dma
        self.deps = []
        self.sig = None
        self.waits = []
        self.consumed = False
        self.pre = None


ENGS = ("pe", "act", "dve", "pool", "sp")
NDMASEM = 6


class Prog:
    def __init__(self, nc):
        self.nc = nc
        self.ops = {e: [] for e in ENGS}
        self.pending = {e: [] for e in ENGS}

    def barrier(self):
        lasts = []
        for e in ENGS:
            ndma = 0
            seen_compute = False
            for o in reversed(self.ops[e]):
                if o.dma:
                    if ndma < NDMASEM:
                        lasts.append(o)
                        ndma += 1
                elif not seen_compute:
                    lasts.append(o)
                    seen_compute = True
                if ndma >= NDMASEM and seen_compute:
                    break
        for e in ENGS:
            self.pending[e] = list(lasts)

    def op(self, eng, fn, reads=(), writes=(), dma=False):
        o = Op(eng, fn, dma)
        deps = set(self.pending[eng])
        self.pending[eng] = []
        for b in reads:
            if b.last_w is not None:
                deps.add(b.last_w)
        for b in writes:
            if b.last_w is not None:
                deps.add(b.last_w)
            deps.update(b.readers)
        for b in writes:
            b.last_w = o
            b.readers = []
        for b in reads:
            b.readers.append(o)
        for d in deps:
            if d is o:
                continue
            if d.eng == "pe" and eng == "pe" and not d.dma and not dma:
                continue
            o.deps.append(d)
            d.consumed = True
        self.ops[eng].append(o)
        return o

    def finalize_and_emit(self, final_wait_ops=()):
        nc = self.nc
        for o in final_wait_ops:
            o.consumed = True
        with contextlib.ExitStack() as st:
            csem = {e: st.enter_context(nc.semaphore("c_" + e)) for e in ENGS}
            dsem = {
                e: [st.enter_context(nc.semaphore("d_%s%d" % (e, i))) for i in range(NDMASEM)]
                for e in ("sp", "pool", "act")
            }
            for e in ENGS:
                cnt = 0
                dcnt = [0] * NDMASEM
                di = 0
                last_on_sem = [None] * NDMASEM
                for o in self.ops[e]:
                    if o.dma:
                        k = di % NDMASEM
                        di += 1
                        dcnt[k] += 16
                        o.sig = (dsem[e][k], dcnt[k])
                        o.pre = last_on_sem[k]
                        last_on_sem[k] = o
                    elif o.consumed:
                        cnt += 1
                        o.sig = (csem[e], cnt)
            for e in ENGS:
                seen = {}
                for o in self.ops[e]:
                    need = {}
                    ds = list(o.deps)
                    if o.pre is not None:
                        ds.append(o.pre)
                    for d in ds:
                        s, v = d.sig
                        key = id(s)
                        if seen.get(key, 0) >= v:
                            continue
                        if key not in need or need[key][1] < v:
                            need[key] = (s, v)
                    for key, (s, v) in need.items():
                        seen[key] = v
                        o.waits.append((s, v))
            finals = [o.sig for o in final_wait_ops]
            ops = self.ops

            def emit(eng, e):
                for o in ops[e]:
                    for s, v in o.waits:
                        eng.wait_ge(s, v)
                    ins = o.fn(eng)
                    if o.sig is not None:
                        ins.then_inc(o.sig[0], 16 if o.dma else 1)
                if e == "sp":
                    done = {}
                    for s, v in finals:
                        if done.get(id(s), (None, 0))[1] < v:
                            done[id(s)] = (s, v)
                    for s, v in done.values():
                        eng.wait_ge(s, v)

            with nc.Block() as block:

                @block.tensor
                def _(eng):
                    emit(eng, "pe")

                @block.scalar
                def _(eng):
                    emit(eng, "act")

                @block.vector
                def _(eng):
                    emit(eng, "dve")

                @block.gpsimd
                def _(eng):
                    emit(eng, "pool")

                @block.sync
                def _(eng):
                    emit(eng, "sp")


class TB:
    def __init__(self, t):
        self.t = t
        self.b = Buf()


class Ring:
    def __init__(self, items):
        self.items = items
        self.i = 0

    def next(self):
        x = self.items[self.i % len(self.items)]
        self.i += 1
        return x


class KB:
    def __init__(self, nc, T):
        self.nc = nc
        self.T = T
        self.P = Prog(nc)
        self.st = contextlib.ExitStack()
        self.ps = Ring([TB(self.st.enter_context(nc.psum_tensor("ps%d" % i, [128, 512], F32))) for i in range(8)])

    def sb(self, name, shape, dt):
        return TB(self.st.enter_context(self.nc.sbuf_tensor(name, shape, dt)))

    def ring(self, name, n, shape, dt):
        return Ring([self.sb("%s%d" % (name, i), shape, dt) for i in range(n)])

    def dma(self, eng, out, in_, R, W):
        return self.P.op(eng, lambda e: e.dma_start(out=out, in_=in_), R, W, dma=True)

    def mm(self, out, lhsT, rhs, start, stop, R, W):
        return self.P.op("pe", lambda e: e.matmul(out, lhsT=lhsT, rhs=rhs, start=start, stop=stop), R, W)

    def tr(self, out, in_, ident, R, W):
        return self.P.op("pe", lambda e: e.transpose(out, in_, ident), R, W)

    def act(self, out, in_, func, R, W, **kw):
        return self.P.op("act", lambda e: e.activation(out=out, in_=in_, func=func, **kw), R, W)

    def tt(self, eng, out, in0, in1, op, R, W):
        return self.P.op(eng, lambda e: e.tensor_tensor(out=out, in0=in0, in1=in1, op=op), R, W)

    def stt(self, eng, out, in0, scalar, in1, op0, op1, R, W):
        return self.P.op(
            eng, lambda e: e.scalar_tensor_tensor(out=out, in0=in0, scalar=scalar, in1=in1, op0=op0, op1=op1), R, W
        )

    def ts(self, eng, out, in0, s1, s2, op0, op1, R, W):
        if op1 is None:
            return self.P.op(eng, lambda e: e.tensor_scalar(out=out, in0=in0, scalar1=s1, scalar2=None, op0=op0), R, W)
        return self.P.op(
            eng, lambda e: e.tensor_scalar(out=out, in0=in0, scalar1=s1, scalar2=s2, op0=op0, op1=op1), R, W
        )

    def cp(self, eng, out, in_, R, W):
        return self.P.op(eng, lambda e: e.tensor_copy(out=out, in_=in_), R, W)

    def memset(self, eng, ap, val, W):
        return self.P.op(eng, lambda e: e.memset(ap, val), (), W)

    def recip(self, out, in_, R, W):
        return self.P.op("dve", lambda e: e.reciprocal(out=out, in_=in_), R, W)


def build_nc(T, n_sub=8, final=True):
    nc = bass.Bass("TRN2", target_bir_lowering=False)
    NT = T // 128
    nslots_total = sum(NSLOT[k] + 16 for k in KINDS)
    x_d = nc.dram_tensor("x", [T, D], F32, kind="ExternalInput").ap()
    y_d = nc.dram_tensor("y", [T, D], F32, kind="ExternalOutput").ap()
    ws_d = nc.dram_tensor("wslots", [nslots_total, 128, 4096], F32, kind="ExternalInput").ap()
    gbc_d = nc.dram_tensor("gbc", [9, 128, D], F32, kind="ExternalInput").ap()
    lrup_d = nc.dram_tensor("lrup", [2, 128, 8, 8], F32, kind="ExternalInput").ap()
    ident_d = nc.dram_tensor("ident", [128, 128], F32, kind="ExternalInput").ap()
    cos_d = nc.dram_tensor("rcos", [T, 128], F32, kind="ExternalInput").ap()
    sin_d = nc.dram_tensor("rsin", [T, 128], F32, kind="ExternalInput").ap()
    rmask_d = nc.dram_tensor("rmask", [128, 4, 128], F32, kind="ExternalInput").ap()
    rdec_d = nc.dram_tensor("rdec", [128, 4, 128], F32, kind="ExternalInput").ap()
    rkdec_d = nc.dram_tensor("rkdec", [128, 4], F32, kind="ExternalInput").ap()
    gx_d = nc.dram_tensor("gdn_wx", [128, 8, 32], F32, kind="ExternalInput").ap()
    gcw_d = nc.dram_tensor("gdn_cw", [128, 32, 4], F32, kind="ExternalInput").ap()
    gvec_d = nc.dram_tensor("gdn_vec", [128, 160], F32, kind="ExternalInput").ap()
    gcon_d = nc.dram_tensor("gdn_con", [128, 5, 128], F32, kind="ExternalInput").ap()
    h_d = nc.dram_tensor("hscr", [T, D], F32, kind="Internal").ap()
    y2_d = nc.dram_tensor("y2scr", [T, 2048], BF16, kind="Internal").ap()

    k = KB(nc, T)
    P = k.P
    with k.st:
        hd = [Buf() for _ in range(NT)]
        y2b = [Buf() for _ in range(NT)]
        ARENA = 178 * 1024
        scr = k.sb("arena", [128, ARENA], mybir.dt.uint8)
        ident_f = k.sb("ident_f", [128, 128], F32)
        ident_b = k.sb("ident_b", [128, 128], BF16)
        gt = k.sb("gt", [128, D], F32)
        unring = k.ring("un", 2, [128, D], BF16)
        junk = k.sb("junk", [128, D], BF16)
        ssr = k.ring("ss", 4, [128, 4], F32)

        class V:
            def __init__(self, t):
                self.t = t
                self.b = Buf()

        st8 = {}

        def begin(nslot, nh, utok):
            P.barrier()
            off = 0
            slots = []
            for i in range(nslot):
                v, off = carve(off, [128, 4096], BF16)
                slots.append(V(v))
            hs = []
            for i in range(nh):
                v, off = carve(off, [128, D], F32)
                hs.append(V(v))
            v, off = carve(off, [128, 8, utok], BF16)
            st8["slots"], st8["hring"], st8["uT"] = slots, Ring(hs), V(v)
            return off

        k.dma("sp", ident_f.t[:], ident_d[:, :], [], [ident_f.b])
        k.dma("pool", ident_b.t[:], ident_d[:, :], [], [ident_b.b])

        def load_slots(base, n):
            slots = st8["slots"]
            for i in range(n):
                for hh in range(2):
                    k.dma("pool", slots[i].t[:, hh * 2048:(hh + 1) * 2048], ws_d[base + i, :, hh * 2048:(hh + 1) * 2048],
                          [], [slots[i].b])

        def carve(off, shape, dt):
            n = int(np.prod(shape[1:]))
            esz = 2 if dt == BF16 else 4
            off = (off + 31) // 32 * 32
            assert off + n * esz <= ARENA, (off, n * esz)
            v = scr.t[:, off:off + n * esz].bitcast(dt)
            if len(shape) == 3:
                v = v.rearrange("p (a b) -> p a b", a=shape[1])
            elif len(shape) == 4:
                v = v.rearrange("p (a b c) -> p a b c", a=shape[1], b=shape[2])
            return v, off + n * esz

        def norm_tile(src_ap, src_buf, hb, dst, dstbuf, col0):
            k.dma("sp", hb.t[:], src_ap, [src_buf], [hb.b])
            ss = ssr.next()
            k.act(junk.t[:], hb.t[:], AF.Square, [hb.b], [junk.b, ss.b], accum_out=ss.t[:, 0:1])
            k.act(ss.t[:, 1:2], ss.t[:, 0:1], AF.Sqrt, [ss.b], [ss.b], scale=1.0 / D, bias=RMS_EPS)
            k.recip(ss.t[:, 2:3], ss.t[:, 1:2], [ss.b], [ss.b])
            un = unring.next()
            k.stt("dve", un.t[:], hb.t[:], ss.t[:, 2:3], gt.t[:], ALU.mult, ALU.mult, [hb.b, ss.b, gt.b], [un.b])
            ps = k.ps.next()
            pv = ps.t[:].bitcast(BF16)
            for c in range(8):
                k.tr(pv[:, c * 128:(c + 1) * 128], un.t[:, c * 128:(c + 1) * 128], ident_b.t[:], [un.b, ident_b.b], [ps.b])
            k.act(dst[:, :, col0:col0 + 128], pv.rearrange("p (c t) -> p c t", c=8), AF.Copy, [ps.b], [dstbuf])

        def finish_tile(hb, t, last):
            if not last:
                return k.dma("sp", h_d[t * 128:(t + 1) * 128, :], hb.t[:], [hb.b], [hd[t]])
            ss = ssr.next()
            k.act(junk.t[:], hb.t[:], AF.Square, [hb.b], [junk.b, ss.b], accum_out=ss.t[:, 0:1])
            k.act(ss.t[:, 1:2], ss.t[:, 0:1], AF.Sqrt, [ss.b], [ss.b], scale=1.0 / D, bias=RMS_EPS)
            k.recip(ss.t[:, 2:3], ss.t[:, 1:2], [ss.b], [ss.b])
            k.stt("dve", hb.t[:], hb.t[:], ss.t[:, 2:3], gt.t[:], ALU.mult, ALU.mult, [hb.b, ss.b, gt.b], [hb.b])
            return k.dma("sp", y_d[t * 128:(t + 1) * 128, :], hb.t[:], [hb.b], [hd[t]])

        out_ops = []

        def mlp(layer, src, base, last):
            off = begin(16, 4, 256)
            slots, hring, uT = st8["slots"], st8["hring"], st8["uT"]
            k.dma("sp", gt.t[:], gbc_d[4 + layer, :, :], [], [gt.b])
            load_slots(base, 16)
            hidv, off = carve(off, [128, 32, 256], BF16)
            hidb = Buf()
            tmps = []
            for i in range(2):
                v, off = carve(off, [128, 512], F32)
                tmps.append((v, Buf()))
            tmpr = Ring(tmps)
            for blk in range(T // 256):
                hbs = []
                for i in range(2):
                    t = blk * 2 + i
                    hb = hring.next()
                    hbs.append(hb)
                    norm_tile(src[t * 128:(t + 1) * 128, :], hd[t], hb, uT.t, uT.b, i * 128)
                for m2 in range(16):
                    ps = k.ps.next()
                    for half in range(2):
                        m = 2 * m2 + half
                        sl = slots[m // 4]
                        wv = sl.t[:].rearrange("p (kc n) -> p kc n", kc=8)
                        for kc in range(8):
                            k.mm(ps.t[:, half * 256:(half + 1) * 256], wv[:, kc, (m % 4) * 128:(m % 4 + 1) * 128],
                                 uT.t[:, kc, 0:256], kc == 0, kc == 7, [sl.b, uT.b], [ps.b])
                    tv, tb = tmpr.next()
                    k.act(tv, ps.t[:], AF.Relu, [ps.b], [tb])
                    k.tt("pool", hidv[:, 2 * m2:2 * m2 + 2, :], tv.rearrange("p (a b) -> p a b", a=2),
                         tv.rearrange("p (a b) -> p a b", a=2), ALU.mult, [tb], [hidb])
                for i in range(2):
                    t = blk * 2 + i
                    hb = hbs[i]
                    for ch in range(2):
                        ps = k.ps.next()
                        for m in range(32):
                            sl = slots[8 + m // 4]
                            wv = sl.t[:].rearrange("p (kc n) -> p kc n", kc=4)
                            k.mm(ps.t[:], hidv[:, m, i * 128:(i + 1) * 128], wv[:, m % 4, ch * 512:(ch + 1) * 512],
                                 m == 0, m == 31, [sl.b, hidb], [ps.b])
                        k.tt("dve", hb.t[:, ch * 512:(ch + 1) * 512], hb.t[:, ch * 512:(ch + 1) * 512], ps.t[:], ALU.add,
                             [hb.b, ps.b], [hb.b])
                    if last:
                        pass
                    o = finish_tile(hb, t, False)
                    out_ops.append(o)

        def lru(j, layer, src, base):
            off = begin(7, 6, 512)
            slots, hring, uT = st8["slots"], st8["hring"], st8["uT"]
            k.dma("sp", gt.t[:], gbc_d[layer, :, :], [], [gt.b])
            load_slots(base, 7)
            lp, off = carve(off, [128, 8, 8], F32)
            lpb = Buf()
            cc, off = carve(off, [128, 8, 4], F32)
            ccb = Buf()
            hst, off = carve(off, [128, 8], F32)
            hstb = Buf()
            xr, off = carve(off, [128, 8, 516], F32)
            xrb = [Buf() for _ in range(8)]
            xc, off = carve(off, [128, 8, 512], F32)
            xcb = [Buf() for _ in range(8)]
            xcbf, off = carve(off, [128, 8, 512], BF16)
            xcbfb = Buf()
            gl, off = carve(off, [128, 8, 512], BF16)
            glb = [Buf() for _ in range(8)]
            yT, off = carve(off, [128, 8, 512], BF16)
            yTb = Buf()
            tr = []
            for i in range(6):
                v, off = carve(off, [128, 512], F32)
                tr.append((v, Buf()))
            k.dma("sp", lp, lrup_d[j, :, :, :], [], [lpb])
            k.act(cc[:, :, 2], lp[:, :, 7], AF.Exp, [lpb], [ccb], scale=-1.0)
            k.act(cc[:, :, 2], cc[:, :, 2], AF.Ln, [ccb], [ccb], bias=1.0)
            k.ts("dve", cc[:, :, 0], cc[:, :, 2], -8.0, None, ALU.mult, None, [ccb], [ccb])
            k.ts("dve", cc[:, :, 1], cc[:, :, 2], -16.0, None, ALU.mult, None, [ccb], [ccb])
            k.memset("dve", hst, 0.0, [hstb])
            k.memset("pool", xr, 0.0, xrb)
            win = [slots[i].t[:].rearrange("p (kc n) -> p kc n", kc=8) for i in range(4)]
            wout = [slots[4 + i].t[:].rearrange("p (kc n) -> p kc n", kc=4) for i in range(2)]
            wg = slots[6].t[:].rearrange("p (w g kc n) -> p w g kc n", w=2, g=4, kc=2)
            for blk in range(T // 512):
                hbs = []
                for i in range(4):
                    t = blk * 4 + i
                    hb = hring.next()
                    hbs.append(hb)
                    norm_tile(src[t * 128:(t + 1) * 128, :], hd[t], hb, uT.t, uT.b, i * 128)
                for jc in range(16):
                    ps = k.ps.next()
                    sl = slots[jc // 4]
                    for kc in range(8):
                        k.mm(ps.t[:], win[jc // 4][:, kc, (jc % 4) * 128:(jc % 4 + 1) * 128], uT.t[:, kc, :],
                             kc == 0, kc == 7, [sl.b, uT.b], [ps.b])
                    if jc < 8:
                        k.act(gl[:, jc, :], ps.t[:], AF.Gelu_apprx_tanh, [ps.b], [glb[jc]])
                    else:
                        c = jc - 8
                        k.act(xr[:, c, 3:515], ps.t[:], AF.Copy, [ps.b], [xrb[c]])
                for c in range(8):
                    eng = "dve"
                    k.ts(eng, xc[:, c, :], xr[:, c, 0:512], lp[:, c, 0:1], lp[:, c, 4:5], ALU.mult, ALU.add,
                         [xrb[c], lpb], [xcb[c]])
                    for kk in range(1, 4):
                        k.stt(eng, xc[:, c, :], xr[:, c, kk:kk + 512], lp[:, c, kk:kk + 1], xc[:, c, :], ALU.mult, ALU.add,
                              [xrb[c], lpb, xcb[c]], [xcb[c]])
                    k.cp(eng, xr[:, c, 0:3], xr[:, c, 512:515], [xrb[c]], [xrb[c]])
                k.act(xcbf, xc, AF.Copy, xcb, [xcbfb])
                for c in range(8):
                    g, cc2 = c // 2, c % 2
                    rv, rb = tr[0]
                    iv, ib = tr[1]
                    av, ab = tr[2]
                    mv, mb = tr[3]
                    uv, ub = tr[4]
                    hv, hb_ = tr[5]
                    for w, (dv, db, bcol) in enumerate(((rv, rb, 5), (iv, ib, 6))):
                        ps = k.ps.next()
                        for kc in range(2):
                            k.mm(ps.t[:], wg[:, w, g, kc, cc2 * 128:(cc2 + 1) * 128], xcbf[:, g * 2 + kc, :], kc == 0, kc == 1,
                                 [slots[6].b, xcbfb], [ps.b])
                        k.act(dv, ps.t[:], AF.Sigmoid, [ps.b, lpb], [db], bias=lp[:, c, bcol:bcol + 1])
                    k.act(av, rv, AF.Exp, [rb, ccb], [ab], scale=cc[:, c, 0:1])
                    k.act(mv, rv, AF.Exp, [rb, ccb], [mb], scale=cc[:, c, 1:2])
                    k.act(mv, mv, AF.Sqrt, [mb], [mb], scale=-1.0, bias=1.0)
                    k.tt("dve", uv, iv, xc[:, c, :], ALU.mult, [ib, xcb[c]], [ub])
                    k.tt("dve", uv, uv, mv, ALU.mult, [ub, mb], [ub])
                    P.op("dve", lambda e, hv=hv, av=av, uv=uv, c=c: e.tensor_tensor_scan(
                        out=hv, data0=av, data1=uv, initial=hst[:, c:c + 1], op0=ALU.mult, op1=ALU.add),
                        [ab, ub, hstb], [hb_])
                    k.cp("dve", hst[:, c:c + 1], hv[:, 511:512], [hb_], [hstb])
                    k.tt("dve", yT[:, c, :], gl[:, c, :], hv, ALU.mult, [glb[c], hb_], [yTb])
                for i in range(4):
                    t = blk * 4 + i
                    hb = hbs[i]
                    for ch in range(2):
                        ps = k.ps.next()
                        for kc in range(8):
                            k.mm(ps.t[:], yT[:, kc, i * 128:(i + 1) * 128], wout[kc // 4][:, kc % 4, ch * 512:(ch + 1) * 512],
                                 kc == 0, kc == 7, [slots[4 + kc // 4].b, yTb], [ps.b])
                        k.tt("dve", hb.t[:, ch * 512:(ch + 1) * 512], hb.t[:, ch * 512:(ch + 1) * 512], ps.t[:], ALU.add,
                             [hb.b, ps.b], [hb.b])
                    out_ops.append(finish_tile(hb, t, False))

        def ret(layer, src, base):
            off = begin(12, 2, 128)
            slots, hring, uT = st8["slots"], st8["hring"], st8["uT"]
            k.dma("sp", gt.t[:], gbc_d[layer, :, :], [], [gt.b])
            load_slots(base, 12)
            mask, off = carve(off, [128, 4, 128], F32)
            dec, off = carve(off, [128, 4, 128], F32)
            kdec, off = carve(off, [128, 4], F32)
            cb = Buf()
            S, off = carve(off, [128, 8, 512], F32)
            Sb = [Buf() for _ in range(8)]
            Sbf, off = carve(off, [128, 8, 512], BF16)
            Sbfb = [Buf() for _ in range(8)]
            cs, off = carve(off, [128, 2, 128], F32)
            csb = Buf()
            qk, off = carve(off, [128, 2048], F32)
            qkb = Buf()
            t1, off = carve(off, [128, 1024], F32)
            t1b = Buf()
            t2, off = carve(off, [128, 1024], F32)
            t2b = Buf()
            rot, off = carve(off, [128, 2048], BF16)
            rotb = Buf()
            kd, off = carve(off, [128, 1024], BF16)
            kdb = Buf()
            qkT, off = carve(off, [128, 16, 128], BF16)
            qkTb = Buf()
            qdT, off = carve(off, [128, 8, 128], BF16)
            qdTb = Buf()
            v, off = carve(off, [128, 2048], BF16)
            vb = Buf()
            sg, off = carve(off, [128, 2048], BF16)
            sgb = Buf()
            sT, off = carve(off, [128, 4, 128], BF16)
            sTb = Buf()
            yv, off = carve(off, [128, 2048], BF16)
            yvb = Buf()
            rs, off = carve(off, [128, 12], F32)
            rsb = Buf()
            k.dma("sp", mask, rmask_d[:, :, :], [], [cb])
            k.dma("sp", dec, rdec_d[:, :, :], [], [cb])
            k.dma("sp", kdec, rkdec_d[:, :], [], [cb])
            k.memset("pool", S, 0.0, Sb)
            k.memset("pool", Sbf, 0.0, Sbfb)
            win = [slots[i].t[:].rearrange("p (kc n) -> p kc n", kc=8) for i in range(12)]
            for t in range(NT):
                hb = hring.next()
                norm_tile(src[t * 128:(t + 1) * 128, :], hd[t], hb, uT.t, uT.b, 0)
                k.dma("sp", cs[:, 0, :], cos_d[t * 128:(t + 1) * 128, :], [], [csb])
                k.dma("sp", cs[:, 1, :], sin_d[t * 128:(t + 1) * 128, :], [], [csb])
                for cbk in range(12):
                    ps = k.ps.next()
                    for kc in range(8):
                        k.mm(ps.t[:], uT.t[:, kc, 0:128], win[cbk][:, kc, :], kc == 0, kc == 7, [slots[cbk].b, uT.b], [ps.b])
                    if cbk < 4:
                        k.act(qk[:, cbk * 512:(cbk + 1) * 512], ps.t[:], AF.Copy, [ps.b], [qkb])
                    elif cbk < 8:
                        k.act(v[:, (cbk - 4) * 512:(cbk - 3) * 512], ps.t[:], AF.Copy, [ps.b], [vb])
                    else:
                        k.act(sg[:, (cbk - 8) * 512:(cbk - 7) * 512], ps.t[:], AF.Silu, [ps.b], [sgb])
                qv = qk.rearrange("p (h two f) -> p h two f", h=8, two=2)
                rv = rot.rearrange("p (h two f) -> p h two f", h=8, two=2)
                x1, x2 = qv[:, :, 0, :], qv[:, :, 1, :]
                cosb = cs[:, 0:1, :].to_broadcast([128, 8, 128])
                sinb = cs[:, 1:2, :].to_broadcast([128, 8, 128])
                a1 = t1.rearrange("p (h f) -> p h f", h=8)
                a2 = t2.rearrange("p (h f) -> p h f", h=8)
                k.tt("dve", a1, x1, cosb, ALU.mult, [qkb, csb], [t1b])
                k.tt("pool", a2, x2, sinb, ALU.mult, [qkb, csb], [t2b])
                k.tt("dve", rv[:, :, 0, :], a1, a2, ALU.subtract, [t1b, t2b], [rotb])
                k.tt("dve", a1, x2, cosb, ALU.mult, [qkb, csb, rotb], [t1b])
                k.tt("pool", a2, x1, sinb, ALU.mult, [qkb, csb, rotb], [t2b])
                k.tt("dve", rv[:, :, 1, :], a1, a2, ALU.add, [t1b, t2b], [rotb])
                k.tt("pool", kd.rearrange("p (h d) -> p h d", h=4), rot[:, 1024:2048].rearrange("p (h d) -> p h d", h=4),
                     kdec.unsqueeze(2).to_broadcast([128, 4, 256]), ALU.mult, [rotb, cb], [kdb])
                for half in range(2):
                    ps = k.ps.next()
                    pv = ps.t[:].bitcast(BF16)
                    for c in range(8):
                        col = half * 1024 + c * 128
                        k.tr(pv[:, c * 128:(c + 1) * 128], rot[:, col:col + 128], ident_b.t[:], [rotb, ident_b.b], [ps.b])
                    k.act(qkT[:, half * 8:(half + 1) * 8, :], pv.rearrange("p (c t) -> p c t", c=8), AF.Copy, [ps.b], [qkTb],
                          scale=(1.0 if half == 0 else 1.0 / 16.0))
                k.tt("dve", qdT.rearrange("p (h c) t -> p h c t", h=4), qkT[:, 0:8, :].rearrange("p (h c) t -> p h c t", h=4),
                     dec.unsqueeze(2).to_broadcast([128, 4, 2, 128]), ALU.mult, [qkTb, cb], [qdTb])
                ps = k.ps.next()
                for h in range(4):
                    for dc in range(2):
                        k.mm(ps.t[:, h * 128:(h + 1) * 128], qkT[:, 8 + h * 2 + dc, :], qkT[:, h * 2 + dc, :], dc == 0, dc == 1,
                             [qkTb], [ps.b])
                k.tt("dve", sT, ps.t[:].rearrange("p (h i) -> p h i", h=4), mask, ALU.mult, [ps.b, cb], [sTb])
                for h in range(4):
                    po = k.ps.next()
                    k.mm(po.t[:], sT[:, h, :], v[:, h * 512:(h + 1) * 512], True, False, [sTb, vb], [po.b])
                    for dc in range(2):
                        k.mm(po.t[:], qdT[:, h * 2 + dc, :], Sbf[:, h * 2 + dc, :], False, dc == 1,
                             [qdTb, Sbfb[h * 2 + dc]], [po.b])
                    k.act(junk.t[:, 0:512], po.t[:], AF.Square, [po.b], [junk.b, rsb], accum_out=rs[:, h:h + 1])
                    k.act(rs[:, 4 + h:5 + h], rs[:, h:h + 1], AF.Sqrt, [rsb], [rsb], scale=1.0 / 512.0, bias=RMS_EPS)
                    k.recip(rs[:, 8 + h:9 + h], rs[:, 4 + h:5 + h], [rsb], [rsb])
                    k.stt("dve", yv[:, h * 512:(h + 1) * 512], po.t[:], rs[:, 8 + h:9 + h], sg[:, h * 512:(h + 1) * 512],
                          ALU.mult, ALU.mult, [po.b, rsb, sgb], [yvb])
                    for dc in range(2):
                        pu = k.ps.next()
                        k.mm(pu.t[:], kd[:, h * 256 + dc * 128:h * 256 + (dc + 1) * 128], v[:, h * 512:(h + 1) * 512], True, True,
                             [kdb, vb], [pu.b])
                        gam = float((1.0 - 2.0 ** (-5.0 - h)) ** 128)
                        i8 = h * 2 + dc
                        k.stt("dve", S[:, i8, :], S[:, i8, :], gam, pu.t[:], ALU.mult, ALU.add, [Sb[i8], pu.b], [Sb[i8]])
                        k.act(Sbf[:, i8, :], S[:, i8, :], AF.Copy, [Sb[i8]], [Sbfb[i8]])
                k.dma("sp", y2_d[t * 128:(t + 1) * 128, :], yv, [yvb], [y2b[t]])

        def outproj(src, base, zlayer=None):
            nz = 4 if zlayer is not None else 0
            off = begin(4 + nz, 3, 128)
            slots, hring, uT = st8["slots"], st8["hring"], st8["uT"]
            if zlayer is not None:
                k.dma("sp", gt.t[:], gbc_d[zlayer, :, :], [], [gt.b])
            load_slots(base, 4 + nz)
            yvs = []
            for i in range(2):
                v_, off = carve(off, [128, 2048], BF16)
                yvs.append(V(v_))
            yvr = Ring(yvs)
            yT, off = carve(off, [128, 16, 128], BF16)
            yTb = Buf()
            szs = []
            for i in range(2):
                v_, off = carve(off, [128, 512], BF16)
                szs.append(V(v_))
            szr = Ring(szs)
            wz = [slots[i].t[:].rearrange("p (kc n) -> p kc n", kc=8) for i in range(nz)]
            wout = [slots[nz + i].t[:].rearrange("p (kc n) -> p kc n", kc=4) for i in range(4)]
            for t in range(NT):
                hb = hring.next()
                yv = yvr.next()
                k.dma("sp", yv.t, y2_d[t * 128:(t + 1) * 128, :], [y2b[t]], [yv.b])
                if zlayer is None:
                    k.dma("sp", hb.t[:], src[t * 128:(t + 1) * 128, :], [hd[t]], [hb.b])
                else:
                    norm_tile(src[t * 128:(t + 1) * 128, :], hd[t], hb, uT.t, uT.b, 0)
                    for zb in range(4):
                        ps = k.ps.next()
                        for kc in range(8):
                            k.mm(ps.t[:], uT.t[:, kc, 0:128], wz[zb][:, kc, :], kc == 0, kc == 7, [slots[zb].b, uT.b], [ps.b])
                        sz = szr.next()
                        k.act(sz.t, ps.t[:], AF.Silu, [ps.b], [sz.b])
                        k.tt("dve", yv.t[:, zb * 512:(zb + 1) * 512], yv.t[:, zb * 512:(zb + 1) * 512], sz.t, ALU.mult,
                             [yv.b, sz.b], [yv.b])
                for half in range(2):
                    ps = k.ps.next()
                    pv = ps.t[:].bitcast(BF16)
                    for c in range(8):
                        col = half * 1024 + c * 128
                        k.tr(pv[:, c * 128:(c + 1) * 128], yv.t[:, col:col + 128], ident_b.t[:], [yv.b, ident_b.b], [ps.b])
                    k.act(yT[:, half * 8:(half + 1) * 8, :], pv.rearrange("p (c t) -> p c t", c=8), AF.Copy, [ps.b], [yTb])
                for ch in range(2):
                    ps = k.ps.next()
                    for ec in range(16):
                        k.mm(ps.t[:], yT[:, ec, :], wout[ec // 4][:, ec % 4, ch * 512:(ch + 1) * 512], ec == 0, ec == 15,
                             [slots[nz + ec // 4].b, yTb], [ps.b])
                    k.tt("dve", hb.t[:, ch * 512:(ch + 1) * 512], hb.t[:, ch * 512:(ch + 1) * 512], ps.t[:], ALU.add,
                         [hb.b, ps.b], [hb.b])
                out_ops.append(finish_tile(hb, t, False))

        def gdn(layer, src, base):
            off = begin(8, 2, 256)
            slots, hring, uT = st8["slots"], st8["hring"], st8["uT"]
            k.dma("sp", gt.t[:], gbc_d[layer, :, :], [], [gt.b])
            load_slots(base, 8)
            cb = Buf()
            wx, off = carve(off, [128, 8, 32], BF16)
            gcw, off = carve(off, [128, 32, 4], F32)
            gvec, off = carve(off, [128, 160], F32)
            gcon, off = carve(off, [128, 5, 128], F32)
            ones, off = carve(off, [128, 128], F32)
            nexpA, off = carve(off, [128, 16], F32)
            halo, off = carve(off, [128, 32, 3], F32)
            halob = [Buf() for _ in range(32)]
            S, off = carve(off, [128, 16, 128], F32)
            Sbf, off = carve(off, [128, 16, 128], BF16)
            Sb = [Buf() for _ in range(4)]
            Sbfb = [Buf() for _ in range(4)]
            kd0, off = carve(off, [128, 4, 128], BF16)
            kd1, off = carve(off, [128, 4, 128], BF16)
            kd0b, kd1b = Buf(), Buf()
            qn0, off = carve(off, [128, 2, 128], BF16)
            qn1, off = carve(off, [128, 2, 128], BF16)
            qn0b, qn1b = Buf(), Buf()
            cT, off = carve(off, [128, 32, 256], BF16)
            cTb = [Buf() for _ in range(32)]
            raws = []
            for i in range(2):
                v_, off = carve(off, [128, 260], F32)
                raws.append(V(v_))
            rawr = Ring(raws)
            accs = []
            for i in range(2):
                v_, off = carve(off, [128, 256], F32)
                accs.append(V(v_))
            accr = Ring(accs)
            ba, off = carve(off, [128, 32], F32)
            bab = Buf()
            gs, off = carve(off, [128, 12, 16], F32)
            gsb = Buf()
            BETA, SP, GG, GC, EXPG, GL, KDC, GT0, GT1, BG, TMP = range(11)

            def cf(shape, dt):
                nonlocal off
                v_, off = carve(off, shape, dt)
                return V(v_)

            tm = cf([128, 8, 128], BF16)
            sq = cf([128, 4, 128], F32)
            l2 = cf([128, 12], F32)
            qkn = cf([128, 4, 128], BF16)
            Dg = cf([128, 4, 128], F32)
            dA = cf([128, 4, 128], F32)
            dT = cf([128, 4, 128], F32)
            tf = cf([128, 4, 128], F32)
            kb = cf([128, 4, 128], BF16)
            kbT = cf([128, 4, 128], BF16)
            qknT = cf([128, 4, 128], BF16)
            Pm = [cf([128, 4, 128], F32) for _ in range(2)]
            PTm = [cf([128, 4, 128], F32) for _ in range(2)]
            RT = cf([128, 4, 128], F32)
            RTb = cf([128, 4, 128], BF16)
            vbt = cf([128, 4, 128], BF16)
            kbg = cf([128, 4, 128], BF16)
            usb = cf([128, 4, 128], F32)
            wT = cf([128, 4, 128], BF16)
            attnT = cf([128, 4, 128], BF16)
            vn = cf([128, 4, 128], BF16)
            osb = cf([128, 4, 128], F32)
            rs = cf([128, 12], F32)
            yvs = [cf([128, 2048], BF16) for _ in range(2)]
            yvr = Ring(yvs)

            k.dma("pool", wx, gx_d[:, :, :], [], [cb])
            k.dma("sp", gcw, gcw_d[:, :, :], [], [cb])
            k.dma("sp", gvec, gvec_d[:, :], [], [cb])
            k.dma("sp", gcon, gcon_d[:, :, :], [], [cb])
            k.memset("pool", ones, 1.0, [cb])
            k.act(nexpA, gvec[:, 0:16], AF.Exp, [cb], [cb])
            k.ts("dve", nexpA, nexpA, -1.0, None, ALU.mult, None, [cb], [cb])
            k.memset("pool", halo, 0.0, halob)
            k.memset("pool", S, 0.0, Sb)
            k.memset("pool", Sbf, 0.0, Sbfb)
            k.memset("pool", kd0, 0.0, [kd0b])
            k.memset("pool", kd1, 0.0, [kd1b])
            k.memset("pool", qn0, 0.0, [qn0b])
            k.memset("pool", qn1, 0.0, [qn1b])
            U_, IND0, IND1, NEGS, NEGT = (gcon[:, i, :] for i in range(5))
            win = [slots[i].t[:].rearrange("p (kc n) -> p kc n", kc=8) for i in range(8)]
            bc4 = lambda ap: ap.unsqueeze(2).to_broadcast([128, 4, 128])
            bch = lambda ap: ap.unsqueeze(1).to_broadcast([128, 4, 128])

            for blk in range(T // 256):
                for i in range(2):
                    t = blk * 2 + i
                    hb = hring.next()
                    norm_tile(src[t * 128:(t + 1) * 128, :], hd[t], hb, uT.t, uT.b, i * 128)
                for ch2 in range(16):
                    ps = k.ps.next()
                    for half in range(2):
                        ch = ch2 * 2 + half
                        for kc in range(8):
                            k.mm(ps.t[:, half * 256:(half + 1) * 256], win[ch // 4][:, kc, (ch % 4) * 128:(ch % 4 + 1) * 128],
                                 uT.t[:, kc, :], kc == 0, kc == 7, [slots[ch // 4].b, uT.b], [ps.b])
                    for half in range(2):
                        ch = ch2 * 2 + half
                        rw = rawr.next()
                        k.cp("pool", rw.t[:, 0:3], halo[:, ch, :], [halob[ch]], [rw.b])
                        k.act(rw.t[:, 3:259], ps.t[:, half * 256:(half + 1) * 256], AF.Copy, [ps.b], [rw.b])
                        ac = accr.next()
                        k.ts("dve", ac.t, rw.t[:, 0:256], gcw[:, ch, 0:1], None, ALU.mult, None, [rw.b, cb], [ac.b])
                        for kk in range(1, 4):
                            k.stt("dve", ac.t, rw.t[:, kk:kk + 256], gcw[:, ch, kk:kk + 1], ac.t, ALU.mult, ALU.add,
                                  [rw.b, cb, ac.b], [ac.b])
                        k.cp("pool", halo[:, ch, :], rw.t[:, 256:259], [rw.b], [halob[ch]])
                        k.act(cT[:, ch, :], ac.t, AF.Silu, [ac.b], [cTb[ch]])
                for i in range(2):
                    t = blk * 2 + i
                    tok = slice(i * 128, (i + 1) * 128)
                    ps = k.ps.next()
                    for kc in range(8):
                        k.mm(ps.t[:, 0:32], uT.t[:, kc, tok], wx[:, kc, :], kc == 0, kc == 7, [cb, uT.b], [ps.b])
                    k.act(ba, ps.t[:, 0:32], AF.Copy, [ps.b], [bab])
                    g_ = lambda j: gs[:, j, :]
                    k.act(g_(BETA), ba[:, 0:16], AF.Sigmoid, [bab], [gsb])
                    k.tt("dve", g_(TMP), ba[:, 16:32], gvec[:, 16:32], ALU.add, [bab, cb], [gsb])
                    k.act(g_(TMP), g_(TMP), AF.Exp, [gsb], [gsb])
                    k.act(g_(SP), g_(TMP), AF.Ln, [gsb], [gsb], bias=1.0)
                    k.tt("dve", g_(GG), g_(SP), nexpA, ALU.mult, [gsb, cb], [gsb])
                    ps = k.ps.next()
                    k.mm(ps.t[:, 0:16], U_, g_(GG), True, True, [cb, gsb], [ps.b])
                    k.mm(ps.t[:, 16:32], IND0, g_(GG), True, True, [cb, gsb], [ps.b])
                    k.mm(ps.t[:, 32:48], IND1, g_(GG), True, True, [cb, gsb], [ps.b])
                    k.act(g_(GC), ps.t[:, 0:16], AF.Copy, [ps.b], [gsb])
                    k.act(g_(EXPG), ps.t[:, 0:16], AF.Exp, [ps.b], [gsb])
                    k.act(g_(GT0), ps.t[:, 16:32], AF.Exp, [ps.b], [gsb])
                    k.act(g_(GT1), ps.t[:, 32:48], AF.Exp, [ps.b], [gsb])
                    k.act(gs[0:64, GL, :], ps.t[0:64, 16:32], AF.Copy, [ps.b], [gsb])
                    k.act(gs[64:128, GL, :], ps.t[64:128, 32:48], AF.Copy, [ps.b], [gsb])
                    k.tt("dve", g_(KDC), g_(GL), g_(GC), ALU.subtract, [gsb], [gsb])
                    k.act(g_(KDC), g_(KDC), AF.Exp, [gsb], [gsb])
                    k.tt("dve", g_(BG), g_(BETA), g_(EXPG), ALU.mult, [gsb], [gsb])
                    yv = yvr.next()
                    for hg in range(4):
                        hs = slice(4 * hg, 4 * hg + 4)
                        chs = [2 * hg, 2 * hg + 1, 8 + 2 * hg, 9 + 2 * hg] + [16 + 4 * hg + j for j in range(4)]
                        ps = k.ps.next()
                        pv = ps.t[:].bitcast(BF16)
                        for j, ch in enumerate(chs):
                            k.tr(pv[:, j * 128:(j + 1) * 128], cT[:, ch, tok], ident_b.t[:], [cTb[ch], ident_b.b], [ps.b])
                        k.act(tm.t, pv.rearrange("p (c t) -> p c t", c=8), AF.Copy, [ps.b], [tm.b])
                        k.tt("dve", sq.t, tm.t[:, 0:4, :], tm.t[:, 0:4, :], ALU.mult, [tm.b], [sq.b])
                        P.op("dve", lambda e: e.tensor_reduce(out=l2.t[:, 0:4], in_=sq.t, axis=AX.X, op=ALU.add), [sq.b], [l2.b])
                        k.act(l2.t[:, 4:8], l2.t[:, 0:4], AF.Sqrt, [l2.b], [l2.b], bias=1e-6)
                        k.recip(l2.t[:, 8:12], l2.t[:, 4:8], [l2.b], [l2.b])
                        k.ts("dve", l2.t[:, 8:10], l2.t[:, 8:10], 128.0 ** -0.5, None, ALU.mult, None, [l2.b], [l2.b])
                        k.tt("dve", qkn.t, tm.t[:, 0:4, :], bc4(l2.t[:, 8:12]), ALU.mult, [tm.b, l2.b], [qkn.b])
                        ps = k.ps.next()
                        pv = ps.t[:].bitcast(BF16)
                        for j in range(4):
                            k.tr(pv[:, j * 128:(j + 1) * 128], qkn.t[:, j, :], ident_b.t[:], [qkn.b, ident_b.b], [ps.b])
                        k.act(qknT.t, pv[:, 0:512].rearrange("p (c t) -> p c t", c=4), AF.Copy, [ps.b], [qknT.b])
                        kn4 = qkn.t[:, 2:4, :].unsqueeze(2).to_broadcast([128, 2, 2, 128])
                        v4 = lambda ap: ap.rearrange("p (a b) d -> p a b d", a=2)
                        sc4 = lambda ap: ap.rearrange("p (a b) -> p a b", a=2).unsqueeze(3).to_broadcast([128, 2, 2, 128])
                        k.tt("dve", v4(kb.t), kn4, sc4(gs[:, BETA, hs]), ALU.mult, [qkn.b, gsb], [kb.b])
                        k.tt("pool", v4(kbg.t), kn4, sc4(gs[:, BG, hs]), ALU.mult, [qkn.b, gsb], [kbg.b])
                        k.tt("pool", v4(kd0[0:64]), kn4[0:64], sc4(gs[:, KDC, hs])[0:64], ALU.mult, [qkn.b, gsb], [kd0b])
                        k.tt("pool", v4(kd1[64:128]), kn4[64:128], sc4(gs[:, KDC, hs])[64:128], ALU.mult, [qkn.b, gsb], [kd1b])
                        k.tt("pool", vbt.t, tm.t[:, 4:8, :], bc4(gs[:, BETA, hs]), ALU.mult, [tm.b, gsb], [vbt.b])
                        ps = k.ps.next()
                        pv = ps.t[:].bitcast(BF16)
                        for j in range(4):
                            k.tr(pv[:, j * 128:(j + 1) * 128], kb.t[:, j, :], ident_b.t[:], [kb.b, ident_b.b], [ps.b])
                        k.act(kbT.t, pv[:, 0:512].rearrange("p (c t) -> p c t", c=4), AF.Copy, [ps.b], [kbT.b])
                        k.tt("dve", Dg.t, bch(ident_f.t[:]), bc4(gs[:, GC, hs]), ALU.mult, [ident_f.b, gsb], [Dg.b])
                        pg = k.ps.next()
                        k.mm(pg.t[:], ones, Dg.t.rearrange("p h c -> p (h c)"), True, True, [cb, Dg.b], [pg.b])
                        pg3 = pg.t[:].rearrange("p (h c) -> p h c", h=4)
                        k.stt("dve", tf.t, pg3, -1.0, bc4(gs[:, GC, hs]), ALU.mult, ALU.add, [pg.b, gsb], [tf.b])
                        k.tt("dve", tf.t, tf.t, bch(NEGS), ALU.add, [tf.b, cb], [tf.b])
                        k.act(dA.t, tf.t, AF.Exp, [tf.b], [dA.b])
                        k.tt("dve", tf.t, pg3, bc4(gs[:, GC, hs]), ALU.subtract, [pg.b, gsb, dA.b], [tf.b])
                        k.tt("dve", tf.t, tf.t, bch(NEGT), ALU.add, [tf.b, cb], [tf.b])
                        k.act(dT.t, tf.t, AF.Exp, [tf.b], [dT.b])
                        ps = k.ps.next()
                        for j in range(4):
                            k.mm(ps.t[:, j * 128:(j + 1) * 128], kbT.t[:, j, :], qknT.t[:, 2 + j // 2, :], True, True,
                                 [kbT.b, qknT.b], [ps.b])
                        cur = 0
                        k.stt("dve", Pm[cur].t, ps.t[:].rearrange("p (h c) -> p h c", h=4), -1.0, dA.t, ALU.mult, ALU.mult,
                              [ps.b, dA.b], [Pm[cur].b])
                        ps = k.ps.next()
                        for j in range(4):
                            k.tr(ps.t[:, j * 128:(j + 1) * 128], Pm[cur].t[:, j, :], ident_f.t[:], [Pm[cur].b, ident_f.b], [ps.b])
                        k.act(PTm[cur].t, ps.t[:].rearrange("p (h c) -> p h c", h=4), AF.Copy, [ps.b], [PTm[cur].b])
                        k.tt("dve", RT.t, PTm[cur].t, bch(ident_f.t[:]), ALU.add, [PTm[cur].b, ident_f.b], [RT.b])
                        for lvl in range(1, 6):
                            nxt = 1 - cur
                            ps = k.ps.next()
                            for j in range(4):
                                k.mm(ps.t[:, j * 128:(j + 1) * 128], PTm[cur].t[:, j, :], Pm[cur].t[:, j, :], True, True,
                                     [PTm[cur].b, Pm[cur].b], [ps.b])
                            k.act(Pm[nxt].t, ps.t[:].rearrange("p (h c) -> p h c", h=4), AF.Copy, [ps.b], [Pm[nxt].b])
                            if lvl < 5:
                                ps = k.ps.next()
                                for j in range(4):
                                    k.mm(ps.t[:, j * 128:(j + 1) * 128], Pm[cur].t[:, j, :], PTm[cur].t[:, j, :], True, True,
                                         [PTm[cur].b, Pm[cur].b], [ps.b])
                                k.cp("dve", PTm[nxt].t, ps.t[:].rearrange("p (h c) -> p h c", h=4), [ps.b], [PTm[nxt].b])
                            ps = k.ps.next()
                            for j in range(4):
                                k.mm(ps.t[:, j * 128:(j + 1) * 128], Pm[nxt].t[:, j, :], RT.t[:, j, :], True, True,
                                     [Pm[nxt].b, RT.b], [ps.b])
                            k.tt("dve", RT.t, RT.t, ps.t[:].rearrange("p (h c) -> p h c", h=4), ALU.add, [RT.b, ps.b], [RT.b])
                            cur = nxt
                        k.act(RTb.t, RT.t, AF.Copy, [RT.b], [RTb.b])
                        ps = k.ps.next()
                        for j in range(4):
                            k.mm(ps.t[:, j * 128:(j + 1) * 128], RTb.t[:, j, :], vbt.t[:, j, :], True, True, [RTb.b, vbt.b], [ps.b])
                        k.act(usb.t, ps.t[:].rearrange("p (h c) -> p h c", h=4), AF.Copy, [ps.b], [usb.b])
                        ps = k.ps.next()
                        for j in range(4):
                            k.mm(ps.t[:, j * 128:(j + 1) * 128], kbg.t[:, j, :], RTb.t[:, j, :], True, True, [RTb.b, kbg.b], [ps.b])
                        k.act(wT.t, ps.t[:].rearrange("p (h c) -> p h c", h=4), AF.Copy, [ps.b], [wT.b])
                        ps = k.ps.next()
                        for j in range(2):
                            k.mm(ps.t[:, j * 128:(j + 1) * 128], qknT.t[:, 2 + j, :], qknT.t[:, j, :], True, True, [qknT.b], [ps.b])
                        k.tt("dve", v4(attnT.t), ps.t[:, 0:256].rearrange("p (a c) -> p a c", a=2).unsqueeze(2).to_broadcast([128, 2, 2, 128]),
                             v4(dT.t), ALU.mult, [ps.b, dT.b], [attnT.b])
                        pq = k.ps.next()
                        pw = k.ps.next()
                        for j in range(4):
                            k.mm(pw.t[:, j * 128:(j + 1) * 128], wT.t[:, j, :], Sbf[:, 4 * hg + j, :], True, True,
                                 [wT.b, Sbfb[hg]], [pw.b])
                        for j in range(4):
                            k.mm(pq.t[:, j * 128:(j + 1) * 128], qknT.t[:, j // 2, :], Sbf[:, 4 * hg + j, :], True, True,
                                 [qknT.b, Sbfb[hg]], [pq.b])
                        k.tt("dve", vn.t, usb.t, pw.t[:].rearrange("p (h c) -> p h c", h=4), ALU.subtract, [usb.b, pw.b], [vn.b])
                        S4 = S[:, hs, :]
                        for cchunk, (kdx, kdxb, gtx) in enumerate(((kd0, kd0b, GT0), (kd1, kd1b, GT1))):
                            pu = k.ps.next()
                            for j in range(4):
                                k.mm(pu.t[:, j * 128:(j + 1) * 128], kdx[:, j, :], vn.t[:, j, :], True, True, [kdxb, vn.b], [pu.b])
                            k.tt("dve", S4, S4, bc4(gs[:, gtx, hs]), ALU.mult, [Sb[hg], gsb], [Sb[hg]])
                            k.tt("dve", S4, S4, pu.t[:].rearrange("p (h c) -> p h c", h=4), ALU.add, [Sb[hg], pu.b], [Sb[hg]])
                            k.act(Sbf[:, hs, :], S4, AF.Copy, [Sb[hg]], [Sbfb[hg]])
                            if cchunk == 0:
                                pw = k.ps.next()
                                for j in range(4):
                                    k.mm(pw.t[:, j * 128:(j + 1) * 128], wT.t[:, j, :], Sbf[:, 4 * hg + j, :], True, True,
                                         [wT.b, Sbfb[hg]], [pw.b])
                                pq1 = k.ps.next()
                                for j in range(4):
                                    k.mm(pq1.t[:, j * 128:(j + 1) * 128], qknT.t[:, j // 2, :], Sbf[:, 4 * hg + j, :], True, True,
                                         [qknT.b, Sbfb[hg]], [pq1.b])
                                k.tt("dve", vn.t[64:128], usb.t[64:128], pw.t[64:128, :].rearrange("p (h c) -> p h c", h=4),
                                     ALU.subtract, [usb.b, pw.b], [vn.b])
                        pa = k.ps.next()
                        for j in range(4):
                            k.mm(pa.t[:, j * 128:(j + 1) * 128], attnT.t[:, j, :], vn.t[:, j, :], True, True, [attnT.b, vn.b], [pa.b])
                        k.tt("dve", osb.t[0:64], pq.t[0:64, :].rearrange("p (h c) -> p h c", h=4), bc4(gs[:, EXPG, hs])[0:64],
                             ALU.mult, [pq.b, gsb], [osb.b])
                        k.tt("dve", osb.t[64:128], pq1.t[64:128, :].rearrange("p (h c) -> p h c", h=4), bc4(gs[:, EXPG, hs])[64:128],
                             ALU.mult, [pq1.b, gsb], [osb.b])
                        k.tt("dve", osb.t, osb.t, pa.t[:].rearrange("p (h c) -> p h c", h=4), ALU.add, [osb.b, pa.b], [osb.b])
                        k.tt("pool", sq.t, osb.t, osb.t, ALU.mult, [osb.b], [sq.b])
                        P.op("dve", lambda e: e.tensor_reduce(out=rs.t[:, 0:4], in_=sq.t, axis=AX.X, op=ALU.add), [sq.b], [rs.b])
                        k.act(rs.t[:, 4:8], rs.t[:, 0:4], AF.Sqrt, [rs.b], [rs.b], scale=1.0 / 128.0, bias=RMS_EPS)
                        k.recip(rs.t[:, 8:12], rs.t[:, 4:8], [rs.b], [rs.b])
                        k.tt("dve", osb.t, osb.t, bc4(rs.t[:, 8:12]), ALU.mult, [osb.b, rs.b], [osb.b])
                        k.tt("dve", yv.t[:, hg * 512:(hg + 1) * 512].rearrange("p (h c) -> p h c", h=4), osb.t, bch(gvec[:, 32:160]),
                             ALU.mult, [osb.b, cb], [yv.b])
                    k.dma("sp", y2_d[t * 128:(t + 1) * 128, :], yv.t, [yv.b], [y2b[t]])

        base = 0
        sub = 0
        cnt = {0: 0, 1: 0, 2: 0}
        for layer in range(4):
            kind = KINDS[layer]
            src = x_d if layer == 0 else h_d
            if sub < n_sub:
                if kind == 0:
                    lru(cnt[0], layer, src, base)
                elif kind == 1:
                    ret(layer, src, base)
                    outproj(src, base + 12)
                else:
                    gdn(layer, src, base)
                    outproj(src, base + 8, zlayer=layer)
            cnt[kind] += 1
            base += NSLOT[kind]
            sub += 1
            if sub < n_sub:
                mlp(layer, h_d, base, False)
            base += 16
            sub += 1
        P.barrier()
        k.dma("sp", gt.t[:], gbc_d[8, :, :], [], [gt.b])
        fin = []
        srcf = h_d if n_sub > 0 else x_d
        begin(0, 4, 128)
        hring = st8["hring"]
        for t in range(NT):
            hb = hring.next()
            k.dma("sp", hb.t[:], srcf[t * 128:(t + 1) * 128, :], [hd[t]], [hb.b])
            if final:
                fin.append(finish_tile(hb, t, True))
            else:
                fin.append(k.dma("sp", y_d[t * 128:(t + 1) * 128, :], hb.t[:], [hb.b], [hd[t]]))
        P.finalize_and_emit(fin)
    return nc


def _colblk(w, c0):
    return np.ascontiguousarray(w[:, c0:c0 + 512].reshape(8, 128, 512).transpose(1, 0, 2)).reshape(128, 4096)


def _rowblk(w, r0):
    return np.ascontiguousarray(w[r0:r0 + 512, :].reshape(4, 128, 1024).transpose(1, 0, 2)).reshape(128, 4096)


def _chan(v):
    return v.reshape(8, 128).T


def pack_inputs(inp, T):
    f = lambda a: np.asarray(a, dtype=np.float32)
    slots = []
    cnt = {0: 0, 1: 0, 2: 0}
    for layer in range(4):
        kind = KINDS[layer]
        j = cnt[kind]
        cnt[kind] += 1
        if kind == 0:
            w_in, w_out = f(inp["lru_w_in"][j]), f(inp["lru_w_out"][j])
            for i in range(4):
                slots.append(_colblk(w_in, i * 512))
            for i in range(2):
                slots.append(_rowblk(w_out, i * 512))
            wa, wx = f(inp["lru_wa"][j]), f(inp["lru_wx"][j])
            g = lambda w: w.reshape(4, 2, 128, 256).transpose(2, 0, 1, 3).reshape(128, 2048)
            slots.append(np.concatenate([g(wa), g(wx)], axis=1))
        elif kind == 1:
            w_in, w_out = f(inp["ret_w_in"][j]), f(inp["ret_w_out"][j])
            for i in range(12):
                slots.append(_colblk(w_in, i * 512))
            for i in range(4):
                slots.append(_rowblk(w_out, i * 512))
        else:
            w_in, w_out = f(inp["gdn_w_in"][j]), f(inp["gdn_w_out"][j])
            for i in range(12):
                slots.append(_colblk(w_in, i * 512))
            for i in range(4):
                slots.append(_rowblk(w_out, i * 512))
        wu, wd = f(inp["mlp_w_up"][layer]), f(inp["mlp_w_down"][layer])
        for i in range(8):
            slots.append(_colblk(wu, i * 512))
        for i in range(8):
            slots.append(_rowblk(wd, i * 512))
    wslots = np.stack(slots, 0)
    gs = [f(inp["mix_norm"][i]) for i in range(4)] + [f(inp["mlp_norm"][i]) for i in range(4)] + [f(inp["final_norm"])]
    gbc = np.stack([np.broadcast_to(g[None, :], (128, D)) for g in gs], 0).copy()
    lrup = np.zeros((2, 128, 8, 8), np.float32)
    for j in range(2):
        cw = f(inp["lru_conv_w"][j])
        for kk in range(4):
            lrup[j, :, :, kk] = _chan(cw[kk])
        lrup[j, :, :, 4] = _chan(f(inp["lru_conv_b"][j]))
        lrup[j, :, :, 5] = _chan(f(inp["lru_ba"][j]).reshape(-1))
        lrup[j, :, :, 6] = _chan(f(inp["lru_bx"][j]).reshape(-1))
        lrup[j, :, :, 7] = _chan(f(inp["lru_lambda"][j]))
    pos = np.arange(T, dtype=np.float32)
    inv_freq = (np.float32(10000.0) ** (-np.arange(0, 256, 2, dtype=np.float32) / np.float32(256))).astype(np.float32)
    ang = (pos[:, None] * inv_freq[None, :]).astype(np.float32)
    rcos, rsin = np.cos(ang).astype(np.float32), np.sin(ang).astype(np.float32)
    lg = np.log1p(-np.exp2(-5.0 - np.arange(4, dtype=np.float64)))
    ii = np.arange(128)
    diff = ii[None, :] - ii[:, None]
    rmask = np.stack([np.where(diff >= 0, np.exp(lg[h] * np.maximum(diff, 0)), 0.0) for h in range(4)], 1).astype(np.float32)
    rdec = np.stack([np.broadcast_to(np.exp(lg[h] * (ii + 1.0))[None, :], (128, 128)) for h in range(4)], 1).astype(np.float32)
    rkdec = np.stack([np.exp(lg[h] * (127.0 - ii)) / 16.0 for h in range(4)], 1).astype(np.float32)
    gw = f(inp["gdn_w_in"][0])
    gdn_wx = np.ascontiguousarray(gw[:, 6144:6176].reshape(8, 128, 32).transpose(1, 0, 2))
    gcw = f(inp["gdn_conv_w"][0])
    gdn_cw = np.ascontiguousarray(gcw.reshape(4, 32, 128).transpose(2, 1, 0))
    gdn_vec = np.zeros((128, 160), np.float32)
    gdn_vec[:, 0:16] = f(inp["gdn_a_log"][0])[None, :]
    gdn_vec[:, 16:32] = f(inp["gdn_dt_bias"][0])[None, :]
    gdn_vec[:, 32:160] = f(inp["gdn_norm"][0])[None, :]
    same = (ii[:, None] // 64) == (ii[None, :] // 64)
    gcon = np.zeros((128, 5, 128), np.float32)
    gcon[:, 0, :] = (same & (ii[:, None] <= ii[None, :]))
    gcon[:, 1, :] = np.broadcast_to((ii < 64)[:, None], (128, 128))
    gcon[:, 2, :] = np.broadcast_to((ii >= 64)[:, None], (128, 128))
    gcon[:, 3, :] = np.where(same & (ii[:, None] > ii[None, :]), 0.0, -30000.0)
    gcon[:, 4, :] = np.where(same & (ii[:, None] <= ii[None, :]), 0.0, -30000.0)
    shared = dict(wslots=wslots, gbc=gbc, lrup=lrup, ident=np.eye(128, dtype=np.float32), rcos=rcos, rsin=rsin,
                  rmask=rmask, rdec=rdec, rkdec=rkdec, gdn_wx=gdn_wx, gdn_cw=gdn_cw, gdn_vec=gdn_vec, gdn_con=gcon)
    return shared


_NC_CACHE = {}


def kernel(**inputs):
    x = np.asarray(inputs["x"], dtype=np.float32)
    B, T, _ = x.shape
    shared = pack_inputs(inputs, T)
    key = (T, 8, True)
    if key not in _NC_CACHE:
        _NC_CACHE[key] = build_nc(T)
    nc = _NC_CACHE[key]
    in_maps = []
    for c in range(NCORES):
        m = dict(shared)
        m["x"] = np.ascontiguousarray(x[c % B])
        in_maps.append(m)
    res = run_bass_kernel_spmd(nc, in_maps, core_ids=list(range(NCORES)))
    return np.stack([np.asarray(res.results[b]["y"], dtype=np.float32) for b in range(B)], 0)
```

```python
import contextlib
import numpy as np
import concourse.bass as bass
import concourse.mybir as mybir
from concourse.bass_utils import run_bass_kernel_spmd

F32 = mybir.dt.float32
BF16 = mybir.dt.bfloat16
AF = mybir.ActivationFunctionType
ALU = mybir.AluOpType
AX = mybir.AxisListType

D = 1024
DFF = 4096
NCORES = 8
RMS_EPS = 1e-6
KINDS = (0, 1, 2, 0)
NSLOT = {0: 7, 1: 16, 2: 16}


class Buf:
    __slots__ = ("last_w", "readers")

    def __init__(self):
        self.last_w = None
        self.readers = []


class Op:
    __slots__ = ("eng", "fn", "deps", "dma", "sig", "waits", "consumed", "pre")

    def __init__(self, eng, fn, dma):
        self.eng = eng
        self.fn = fn
        self.dma = dma
        self.deps = []
        self.sig = None
        self.waits = []
        self.consumed = False
        self.pre = None


ENGS = ("pe", "act", "dve", "pool", "sp")
NDMASEM = 6


class Prog:
    def __init__(self, nc):
        self.nc = nc
        self.ops = {e: [] for e in ENGS}
        self.pending = {e: [] for e in ENGS}

    def barrier(self):
        lasts = []
        for e in ENGS:
            ndma = 0
            seen_compute = False
            for o in reversed(self.ops[e]):
                if o.dma:
                    if ndma < NDMASEM:
                        lasts.append(o)
                        ndma += 1
                elif not seen_compute:
                    lasts.append(o)
                    seen_compute = True
                if ndma >= NDMASEM and seen_compute:
                    break
        for e in ENGS:
            self.pending[e] = list(lasts)

    def op(self, eng, fn, reads=(), writes=(), dma=False):
        o = Op(eng, fn, dma)
        deps = set(self.pending[eng])
        self.pending[eng] = []
        for b in reads:
            if b.last_w is not None:
                deps.add(b.last_w)
        for b in writes:
            if b.last_w is not None:
                deps.add(b.last_w)
            deps.update(b.readers)
        for b in writes:
            b.last_w = o
            b.readers = []
        for b in reads:
            b.readers.append(o)
        for d in deps:
            if d is o:
                continue
            if d.eng == "pe" and eng == "pe" and not d.dma and not dma:
                continue
            o.deps.append(d)
            d.consumed = True
        self.ops[eng].append(o)
        return o

    def finalize_and_emit(self, final_wait_ops=()):
        nc = self.nc
        for o in final_wait_ops:
            o.consumed = True
        with contextlib.ExitStack() as st:
            csem = {e: st.enter_context(nc.semaphore("c_" + e)) for e in ENGS}
            dsem = {
                e: [st.enter_context(nc.semaphore("d_%s%d" % (e, i))) for i in range(NDMASEM)]
                for e in ("sp", "pool", "act")
            }
            for e in ENGS:
                cnt = 0
                dcnt = [0] * NDMASEM
                di = 0
                last_on_sem = [None] * NDMASEM
                for o in self.ops[e]:
                    if o.dma:
                        k = di % NDMASEM
                        di += 1
                        dcnt[k] += 16
                        o.sig = (dsem[e][k], dcnt[k])
                        o.pre = last_on_sem[k]
                        last_on_sem[k] = o
                    elif o.consumed:
                        cnt += 1
                        o.sig = (csem[e], cnt)
            for e in ENGS:
                seen = {}
                for o in self.ops[e]:
                    need = {}
                    ds = list(o.deps)
                    if o.pre is not None:
                        ds.append(o.pre)
                    for d in ds:
                        s, v = d.sig
                        key = id(s)
                        if seen.get(key, 0) >= v:
                            continue
                        if key not in need or need[key][1] < v:
                            need[key] = (s, v)
                    for key, (s, v) in need.items():
                        seen[key] = v
                        o.waits.append((s, v))
            finals = [o.sig for o in final_wait_ops]
            ops = self.ops

            def emit(eng, e):
                for o in ops[e]:
                    for s, v in o.waits:
                        eng.wait_ge(s, v)
                    ins = o.fn(eng)
                    if o.sig is not None:
                        ins.then_inc(o.sig[0], 16 if o.dma else 1)
                if e == "sp":
                    done = {}
                    for s, v in finals:
                        if done.get(id(s), (None, 0))[1] < v:
                            done[id(s)] = (s, v)
                    for s, v in done.values():
                        eng.wait_ge(s, v)

            with nc.Block() as block:

                @block.tensor
                def _(eng):
                    emit(eng, "pe")

                @block.scalar
                def _(eng):
                    emit(eng, "act")

                @block.vector
                def _(eng):
                    emit(eng, "dve")

                @block.gpsimd
                def _(eng):
                    emit(eng, "pool")

                @block.sync
                def _(eng):
                    emit(eng, "sp")


class TB:
    def __init__(self, t):
        self.t = t
        self.b = Buf()


class Ring:
    def __init__(self, items):
        self.items = items
        self.i = 0

    def next(self):
        x = self.items[self.i % len(self.items)]
        self.i += 1
        return x


class KB:
    def __init__(self, nc, T):
        self.nc = nc
        self.T = T
        self.P = Prog(nc)
        self.st = contextlib.ExitStack()
        self.ps = Ring([TB(self.st.enter_context(nc.psum_tensor("ps%d" % i, [128, 512], F32))) for i in range(8)])

    def sb(self, name, shape, dt):
        return TB(self.st.enter_context(self.nc.sbuf_tensor(name, shape, dt)))

    def ring(self, name, n, shape, dt):
        return Ring([self.sb("%s%d" % (name, i), shape, dt) for i in range(n)])

    def dma(self, eng, out, in_, R, W):
        return self.P.op(eng, lambda e: e.dma_start(out=out, in_=in_), R, W, dma=True)

    def mm(self, out, lhsT, rhs, start, stop, R, W):
        return self.P.op("pe", lambda e: e.matmul(out, lhsT=lhsT, rhs=rhs, start=start, stop=stop), R, W)

    def tr(self, out, in_, ident, R, W):
        return self.P.op("pe", lambda e: e.transpose(out, in_, ident), R, W)

    def act(self, out, in_, func, R, W, **kw):
        return self.P.op("act", lambda e: e.activation(out=out, in_=in_, func=func, **kw), R, W)

    def tt(self, eng, out, in0, in1, op, R, W):
        return self.P.op(eng, lambda e: e.tensor_tensor(out=out, in0=in0, in1=in1, op=op), R, W)

    def stt(self, eng, out, in0, scalar, in1, op0, op1, R, W):
        return self.P.op(
            eng, lambda e: e.scalar_tensor_tensor(out=out, in0=in0, scalar=scalar, in1=in1, op0=op0, op1=op1), R, W
        )

    def ts(self, eng, out, in0, s1, s2, op0, op1, R, W):
        if op1 is None:
            return self.P.op(eng, lambda e: e.tensor_scalar(out=out, in0=in0, scalar1=s1, scalar2=None, op0=op0), R, W)
        return self.P.op(
            eng, lambda e: e.tensor_scalar(out=out, in0=in0, scalar1=s1, scalar2=s2, op0=op0, op1=op1), R, W
        )

    def cp(self, eng, out, in_, R, W):
        return self.P.op(eng, lambda e: e.tensor_copy(out=out, in_=in_), R, W)

    def memset(self, eng, ap, val, W):
        return self.P.op(eng, lambda e: e.memset(ap, val), (), W)

    def recip(self, out, in_, R, W):
        return self.P.op("dve", lambda e: e.reciprocal(out=out, in_=in_), R, W)


def build_nc(T, n_sub=8, final=True):
    nc = bass.Bass("TRN2", target_bir_lowering=False)
    NT = T // 128
    nslots_total = sum(NSLOT[k] + 16 for k in KINDS)
    x_d = nc.dram_tensor("x", [T, D], F32, kind="ExternalInput").ap()
    y_d = nc.dram_tensor("y", [T, D], F32, kind="ExternalOutput").ap()
    ws_d = nc.dram_tensor("wslots", [nslots_total, 128, 4096], F32, kind="ExternalInput").ap()
    gbc_d = nc.dram_tensor("gbc", [9, 128, D], F32, kind="ExternalInput").ap()
    lrup_d = nc.dram_tensor("lrup", [2, 128, 8, 8], F32, kind="ExternalInput").ap()
    ident_d = nc.dram_tensor("ident", [128, 128], F32, kind="ExternalInput").ap()
    cos_d = nc.dram_tensor("rcos", [T, 128], F32, kind="ExternalInput").ap()
    sin_d = nc.dram_tensor("rsin", [T, 128], F32, kind="ExternalInput").ap()
    rmask_d = nc.dram_tensor("rmask", [128, 4, 128], F32, kind="ExternalInput").ap()
    rdec_d = nc.dram_tensor("rdec", [128, 4, 128], F32, kind="ExternalInput").ap()
    rkdec_d = nc.dram_tensor("rkdec", [128, 4], F32, kind="ExternalInput").ap()
    gx_d = nc.dram_tensor("gdn_wx", [128, 8, 32], F32, kind="ExternalInput").ap()
    gcw_d = nc.dram_tensor("gdn_cw", [128, 32, 4], F32, kind="ExternalInput").ap()
    gvec_d = nc.dram_tensor("gdn_vec", [128, 160], F32, kind="ExternalInput").ap()
    gcon_d = nc.dram_tensor("gdn_con", [128, 5, 128], F32, kind="ExternalInput").ap()
    h_d = nc.dram_tensor("hscr", [T, D], F32, kind="Internal").ap()
    y2_d = nc.dram_tensor("y2scr", [T, 2048], BF16, kind="Internal").ap()

    k = KB(nc, T)
    P = k.P
    with k.st:
        hd = [Buf() for _ in range(NT)]
        y2b = [Buf() for _ in range(NT)]
        ARENA = 178 * 1024
        scr = k.sb("arena", [128, ARENA], mybir.dt.uint8)
        ident_f = k.sb("ident_f", [128, 128], F32)
        ident_b = k.sb("ident_b", [128, 128], BF16)
        gt = k.sb("gt", [128, D], F32)
        unring = k.ring("un", 2, [128, D], BF16)
        junk = k.sb("junk", [128, D], BF16)
        ssr = k.ring("ss", 4, [128, 4], F32)

        class V:
            def __init__(self, t):
                self.t = t
                self.b = Buf()

        st8 = {}

        def begin(nslot, nh, utok):
            P.barrier()
            off = 0
            slots = []
            for i in range(nslot):
                v, off = carve(off, [128, 4096], BF16)
                slots.append(V(v))
            hs = []
            for i in range(nh):
                v, off = carve(off, [128, D], F32)
                hs.append(V(v))
            v, off = carve(off, [128, 8, utok], BF16)
            st8["slots"], st8["hring"], st8["uT"] = slots, Ring(hs), V(v)
            return off

        k.dma("sp", ident_f.t[:], ident_d[:, :], [], [ident_f.b])
        k.dma("pool", ident_b.t[:], ident_d[:, :], [], [ident_b.b])

        def load_slots(base, n):
            slots = st8["slots"]
            for i in range(n):
                for hh in range(2):
                    k.dma("pool", slots[i].t[:, hh * 2048:(hh + 1) * 2048], ws_d[base + i, :, hh * 2048:(hh + 1) * 2048],
                          [], [slots[i].b])

        def carve(off, shape, dt):
            n = int(np.prod(shape[1:]))
            esz = 2 if dt == BF16 else 4
            off = (off + 31) // 32 * 32
            assert off + n * esz <= ARENA, (off, n * esz)
            v = scr.t[:, off:off + n * esz].bitcast(dt)
            if len(shape) == 3:
                v = v.rearrange("p (a b) -> p a b", a=shape[1])
            elif len(shape) == 4:
                v = v.rearrange("p (a b c) -> p a b c", a=shape[1], b=shape[2])
            return v, off + n * esz

        def norm_tile(src_ap, src_buf, hb, dst, dstbuf, col0):
            k.dma("sp", hb.t[:], src_ap, [src_buf], [hb.b])
            ss = ssr.next()
            k.act(junk.t[:], hb.t[:], AF.Square, [hb.b], [junk.b, ss.b], accum_out=ss.t[:, 0:1])
            k.act(ss.t[:, 1:2], ss.t[:, 0:1], AF.Sqrt, [ss.b], [ss.b], scale=1.0 / D, bias=RMS_EPS)
            k.recip(ss.t[:, 2:3], ss.t[:, 1:2], [ss.b], [ss.b])
            un = unring.next()
            k.stt("dve", un.t[:], hb.t[:], ss.t[:, 2:3], gt.t[:], ALU.mult, ALU.mult, [hb.b, ss.b, gt.b], [un.b])
            ps = k.ps.next()
            pv = ps.t[:].bitcast(BF16)
            for c in range(8):
                k.tr(pv[:, c * 128:(c + 1) * 128], un.t[:, c * 128:(c + 1) * 128], ident_b.t[:], [un.b, ident_b.b], [ps.b])
            k.act(dst[:, :, col0:col0 + 128], pv.rearrange("p (c t) -> p c t", c=8), AF.Copy, [ps.b], [dstbuf])

        def finish_tile(hb, t, last):
            if not last:
                return k.dma("sp", h_d[t * 128:(t + 1) * 128, :], hb.t[:], [hb.b], [hd[t]])
            ss = ssr.next()
            k.act(junk.t[:], hb.t[:], AF.Square, [hb.b], [junk.b, ss.b], accum_out=ss.t[:, 0:1])
            k.act(ss.t[:, 1:2], ss.t[:, 0:1], AF.Sqrt, [ss.b], [ss.b], scale=1.0 / D, bias=RMS_EPS)
            k.recip(ss.t[:, 2:3], ss.t[:, 1:2], [ss.b], [ss.b])
            gt2 = st8["gt2"]
            k.stt("dve", hb.t[:], hb.t[:], ss.t[:, 2:3], gt2.t, ALU.mult, ALU.mult, [hb.b, ss.b, gt2.b], [hb.b])
            return k.dma("sp", y_d[t * 128:(t + 1) * 128, :], hb.t[:], [hb.b], [hd[t]])

        out_ops = []
        fin_ops = []

        def load_gt2(off):
            v_, off = carve(off, [128, D], F32)
            g2 = V(v_)
            k.dma("sp", g2.t, gbc_d[8, :, :], [], [g2.b])
            st8["gt2"] = g2
            return off

        def mlp(layer, src, base, last):
            off = begin(16, 4, 256)
            slots, hring, uT = st8["slots"], st8["hring"], st8["uT"]
            k.dma("sp", gt.t[:], gbc_d[4 + layer, :, :], [], [gt.b])
            load_slots(base, 16)
            if last:
                off = load_gt2(off)
            hidv, off = carve(off, [128, 32, 256], BF16)
            hidb = Buf()
            tmps = []
            for i in range(2):
                v, off = carve(off, [128, 512], F32)
                tmps.append((v, Buf()))
            tmpr = Ring(tmps)
            for blk in range(T // 256):
                hbs = []
                for i in range(2):
                    t = blk * 2 + i
                    hb = hring.next()
                    hbs.append(hb)
                    norm_tile(src[t * 128:(t + 1) * 128, :], hd[t], hb, uT.t, uT.b, i * 128)
                for m2 in range(16):
                    ps = k.ps.next()
                    for half in range(2):
                        m = 2 * m2 + half
                        sl = slots[m // 4]
                        wv = sl.t[:].rearrange("p (kc n) -> p kc n", kc=8)
                        for kc in range(8):
                            k.mm(ps.t[:, half * 256:(half + 1) * 256], wv[:, kc, (m % 4) * 128:(m % 4 + 1) * 128],
                                 uT.t[:, kc, 0:256], kc == 0, kc == 7, [sl.b, uT.b], [ps.b])
                    tv, tb = tmpr.next()
                    k.act(tv, ps.t[:], AF.Relu, [ps.b], [tb])
                    k.tt("pool", hidv[:, 2 * m2:2 * m2 + 2, :], tv.rearrange("p (a b) -> p a b", a=2),
                         tv.rearrange("p (a b) -> p a b", a=2), ALU.mult, [tb], [hidb])
                for i in range(2):
                    t = blk * 2 + i
                    hb = hbs[i]
                    for ch in range(2):
                        ps = k.ps.next()
                        for m in range(32):
                            sl = slots[8 + m // 4]
                            wv = sl.t[:].rearrange("p (kc n) -> p kc n", kc=4)
                            k.mm(ps.t[:], hidv[:, m, i * 128:(i + 1) * 128], wv[:, m % 4, ch * 512:(ch + 1) * 512],
                                 m == 0, m == 31, [sl.b, hidb], [ps.b])
                        k.tt("dve", hb.t[:, ch * 512:(ch + 1) * 512], hb.t[:, ch * 512:(ch + 1) * 512], ps.t[:], ALU.add,
                             [hb.b, ps.b], [hb.b])
                    o = finish_tile(hb, t, last)
                    out_ops.append(o)
                    if last:
                        fin_ops.append(o)

        def lru(j, layer, src, base):
            off = begin(7, 6, 512)
            slots, hring, uT = st8["slots"], st8["hring"], st8["uT"]
            k.dma("sp", gt.t[:], gbc_d[layer, :, :], [], [gt.b])
            load_slots(base, 7)
            lp, off = carve(off, [128, 8, 8], F32)
            lpb = Buf()
            cc, off = carve(off, [128, 8, 4], F32)
            ccb = Buf()
            hst, off = carve(off, [128, 8], F32)
            hstb = Buf()
            xr, off = carve(off, [128, 8, 516], F32)
            xrb = [Buf() for _ in range(8)]
            xc, off = carve(off, [128, 8, 512], F32)
            xcb = [Buf() for _ in range(8)]
            xcbf, off = carve(off, [128, 8, 512], BF16)
            xcbfb = Buf()
            gl, off = carve(off, [128, 8, 512], BF16)
            glb = [Buf() for _ in range(8)]
            yT, off = carve(off, [128, 8, 512], BF16)
            yTb = Buf()
            tr = []
            for i in range(6):
                v, off = carve(off, [128, 512], F32)
                tr.append((v, Buf()))
            k.dma("sp", lp, lrup_d[j, :, :, :], [], [lpb])
            k.act(cc[:, :, 2], lp[:, :, 7], AF.Exp, [lpb], [ccb], scale=-1.0)
            k.act(cc[:, :, 2], cc[:, :, 2], AF.Ln, [ccb], [ccb], bias=1.0)
            k.ts("dve", cc[:, :, 0], cc[:, :, 2], -8.0, None, ALU.mult, None, [ccb], [ccb])
            k.ts("dve", cc[:, :, 1], cc[:, :, 2], -16.0, None, ALU.mult, None, [ccb], [ccb])
            k.memset("dve", hst, 0.0, [hstb])
            k.memset("pool", xr, 0.0, xrb)
            win = [slots[i].t[:].rearrange("p (kc n) -> p kc n", kc=8) for i in range(4)]
            wout = [slots[4 + i].t[:].rearrange("p (kc n) -> p kc n", kc=4) for i in range(2)]
            wg = slots[6].t[:].rearrange("p (w g kc n) -> p w g kc n", w=2, g=4, kc=2)
            for blk in range(T // 512):
                hbs = []
                for i in range(4):
                    t = blk * 4 + i
                    hb = hring.next()
                    hbs.append(hb)
                    norm_tile(src[t * 128:(t + 1) * 128, :], hd[t], hb, uT.t, uT.b, i * 128)
                for jc in range(16):
                    ps = k.ps.next()
                    sl = slots[jc // 4]
                    for kc in range(8):
                        k.mm(ps.t[:], win[jc // 4][:, kc, (jc % 4) * 128:(jc % 4 + 1) * 128], uT.t[:, kc, :],
                             kc == 0, kc == 7, [sl.b, uT.b], [ps.b])
                    if jc < 8:
                        k.act(gl[:, jc, :], ps.t[:], AF.Gelu_apprx_tanh, [ps.b], [glb[jc]])
                    else:
                        c = jc - 8
                        k.act(xr[:, c, 3:515], ps.t[:], AF.Copy, [ps.b], [xrb[c]])
                for c in range(8):
                    eng = "dve"
                    k.ts(eng, xc[:, c, :], xr[:, c, 0:512], lp[:, c, 0:1], lp[:, c, 4:5], ALU.mult, ALU.add,
                         [xrb[c], lpb], [xcb[c]])
                    for kk in range(1, 4):
                        k.stt(eng, xc[:, c, :], xr[:, c, kk:kk + 512], lp[:, c, kk:kk + 1], xc[:, c, :], ALU.mult, ALU.add,
                              [xrb[c], lpb, xcb[c]], [xcb[c]])
                    k.cp(eng, xr[:, c, 0:3], xr[:, c, 512:515], [xrb[c]], [xrb[c]])
                k.act(xcbf, xc, AF.Copy, xcb, [xcbfb])
                for c in range(8):
                    g, cc2 = c // 2, c % 2
                    rv, rb = tr[0]
                    iv, ib = tr[1]
                    av, ab = tr[2]
                    mv, mb = tr[3]
                    uv, ub = tr[4]
                    hv, hb_ = tr[5]
                    for w, (dv, db, bcol) in enumerate(((rv, rb, 5), (iv, ib, 6))):
                        ps = k.ps.next()
                        for kc in range(2):
                            k.mm(ps.t[:], wg[:, w, g, kc, cc2 * 128:(cc2 + 1) * 128], xcbf[:, g * 2 + kc, :], kc == 0, kc == 1,
                                 [slots[6].b, xcbfb], [ps.b])
                        k.act(dv, ps.t[:], AF.Sigmoid, [ps.b, lpb], [db], bias=lp[:, c, bcol:bcol + 1])
                    k.act(av, rv, AF.Exp, [rb, ccb], [ab], scale=cc[:, c, 0:1])
                    k.act(mv, rv, AF.Exp, [rb, ccb], [mb], scale=cc[:, c, 1:2])
                    k.act(mv, mv, AF.Sqrt, [mb], [mb], scale=-1.0, bias=1.0)
                    k.tt("dve", uv, iv, xc[:, c, :], ALU.mult, [ib, xcb[c]], [ub])
                    k.tt("dve", uv, uv, mv, ALU.mult, [ub, mb], [ub])
                    P.op("dve", lambda e, hv=hv, av=av, uv=uv, c=c: e.tensor_tensor_scan(
                        out=hv, data0=av, data1=uv, initial=hst[:, c:c + 1], op0=ALU.mult, op1=ALU.add),
                        [ab, ub, hstb], [hb_])
                    k.cp("dve", hst[:, c:c + 1], hv[:, 511:512], [hb_], [hstb])
                    k.tt("dve", yT[:, c, :], gl[:, c, :], hv, ALU.mult, [glb[c], hb_], [yTb])
                for i in range(4):
                    t = blk * 4 + i
                    hb = hbs[i]
                    for ch in range(2):
                        ps = k.ps.next()
                        for kc in range(8):
                            k.mm(ps.t[:], yT[:, kc, i * 128:(i + 1) * 128], wout[kc // 4][:, kc % 4, ch * 512:(ch + 1) * 512],
                                 kc == 0, kc == 7, [slots[4 + kc // 4].b, yTb], [ps.b])
                        k.tt("dve", hb.t[:, ch * 512:(ch + 1) * 512], hb.t[:, ch * 512:(ch + 1) * 512], ps.t[:], ALU.add,
                             [hb.b, ps.b], [hb.b])
                    out_ops.append(finish_tile(hb, t, False))

        def ret(layer, src, base):
            off = begin(12, 2, 128)
            slots, hring, uT = st8["slots"], st8["hring"], st8["uT"]
            k.dma("sp", gt.t[:], gbc_d[layer, :, :], [], [gt.b])
            load_slots(base, 12)
            mask, off = carve(off, [128, 4, 128], F32)
            dec, off = carve(off, [128, 4, 128], F32)
            kdec, off = carve(off, [128, 4], F32)
            cb = Buf()
            S, off = carve(off, [128, 8, 512], F32)
            Sb = [Buf() for _ in range(8)]
            Sbf, off = carve(off, [128, 8, 512], BF16)
            Sbfb = [Buf() for _ in range(8)]
            cs, off = carve(off, [128, 2, 128], F32)
            csb = Buf()
            qk, off = carve(off, [128, 2048], F32)
            qkb = Buf()
            t1, off = carve(off, [128, 1024], F32)
            t1b = Buf()
            t2, off = carve(off, [128, 1024], F32)
            t2b = Buf()
            rot, off = carve(off, [128, 2048], BF16)
            rotb = Buf()
            kd, off = carve(off, [128, 1024], BF16)
            kdb = Buf()
            qkT, off = carve(off, [128, 16, 128], BF16)
            qkTb = Buf()
            qdT, off = carve(off, [128, 8, 128], BF16)
            qdTb = Buf()
            v, off = carve(off, [128, 2048], BF16)
            vb = Buf()
            sg, off = carve(off, [128, 2048], BF16)
            sgb = Buf()
            sT, off = carve(off, [128, 4, 128], BF16)
            sTb = Buf()
            yv, off = carve(off, [128, 2048], BF16)
            yvb = Buf()
            rs, off = carve(off, [128, 12], F32)
            rsb = Buf()
            k.dma("sp", mask, rmask_d[:, :, :], [], [cb])
            k.dma("sp", dec, rdec_d[:, :, :], [], [cb])
            k.dma("sp", kdec, rkdec_d[:, :], [], [cb])
            k.memset("pool", S, 0.0, Sb)
            k.memset("pool", Sbf, 0.0, Sbfb)
            win = [slots[i].t[:].rearrange("p (kc n) -> p kc n", kc=8) for i in range(12)]
            for t in range(NT):
                hb = hring.next()
                norm_tile(src[t * 128:(t + 1) * 128, :], hd[t], hb, uT.t, uT.b, 0)
                k.dma("sp", cs[:, 0, :], cos_d[t * 128:(t + 1) * 128, :], [], [csb])
                k.dma("sp", cs[:, 1, :], sin_d[t * 128:(t + 1) * 128, :], [], [csb])
                for cbk in range(12):
                    ps = k.ps.next()
                    for kc in range(8):
                        k.mm(ps.t[:], uT.t[:, kc, 0:128], win[cbk][:, kc, :], kc == 0, kc == 7, [slots[cbk].b, uT.b], [ps.b])
                    if cbk < 4:
                        k.act(qk[:, cbk * 512:(cbk + 1) * 512], ps.t[:], AF.Copy, [ps.b], [qkb])
                    elif cbk < 8:
                        k.act(v[:, (cbk - 4) * 512:(cbk - 3) * 512], ps.t[:], AF.Copy, [ps.b], [vb])
                    else:
                        k.act(sg[:, (cbk - 8) * 512:(cbk - 7) * 512], ps.t[:], AF.Silu, [ps.b], [sgb])
                qv = qk.rearrange("p (h two f) -> p h two f", h=8, two=2)
                rv = rot.rearrange("p (h two f) -> p h two f", h=8, two=2)
                x1, x2 = qv[:, :, 0, :], qv[:, :, 1, :]
                cosb = cs[:, 0:1, :].to_broadcast([128, 8, 128])
                sinb = cs[:, 1:2, :].to_broadcast([128, 8, 128])
                a1 = t1.rearrange("p (h f) -> p h f", h=8)
                a2 = t2.rearrange("p (h f) -> p h f", h=8)
                k.tt("dve", a1, x1, cosb, ALU.mult, [qkb, csb], [t1b])
                k.tt("pool", a2, x2, sinb, ALU.mult, [qkb, csb], [t2b])
                k.tt("dve", rv[:, :, 0, :], a1, a2, ALU.subtract, [t1b, t2b], [rotb])
                k.tt("dve", a1, x2, cosb, ALU.mult, [qkb, csb, rotb], [t1b])
                k.tt("pool", a2, x1, sinb, ALU.mult, [qkb, csb, rotb], [t2b])
                k.tt("dve", rv[:, :, 1, :], a1, a2, ALU.add, [t1b, t2b], [rotb])
                k.tt("pool", kd.rearrange("p (h d) -> p h d", h=4), rot[:, 1024:2048].rearrange("p (h d) -> p h d", h=4),
                     kdec.unsqueeze(2).to_broadcast([128, 4, 256]), ALU.mult, [rotb, cb], [kdb])
                for half in range(2):
                    ps = k.ps.next()
                    pv = ps.t[:].bitcast(BF16)
                    for c in range(8):
                        col = half * 1024 + c * 128
                        k.tr(pv[:, c * 128:(c + 1) * 128], rot[:, col:col + 128], ident_b.t[:], [rotb, ident_b.b], [ps.b])
                    k.act(qkT[:, half * 8:(half + 1) * 8, :], pv.rearrange("p (c t) -> p c t", c=8), AF.Copy, [ps.b], [qkTb],
                          scale=(1.0 if half == 0 else 1.0 / 16.0))
                k.tt("dve", qdT.rearrange("p (h c) t -> p h c t", h=4), qkT[:, 0:8, :].rearrange("p (h c) t -> p h c t", h=4),
                     dec.unsqueeze(2).to_broadcast([128, 4, 2, 128]), ALU.mult, [qkTb, cb], [qdTb])
                ps = k.ps.next()
                for h in range(4):
                    for dc in range(2):
                        k.mm(ps.t[:, h * 128:(h + 1) * 128], qkT[:, 8 + h * 2 + dc, :], qkT[:, h * 2 + dc, :], dc == 0, dc == 1,
                             [qkTb], [ps.b])
                k.tt("dve", sT, ps.t[:].rearrange("p (h i) -> p h i", h=4), mask, ALU.mult, [ps.b, cb], [sTb])
                for h in range(4):
                    po = k.ps.next()
                    k.mm(po.t[:], sT[:, h, :], v[:, h * 512:(h + 1) * 512], True, False, [sTb, vb], [po.b])
                    for dc in range(2):
                        k.mm(po.t[:], qdT[:, h * 2 + dc, :], Sbf[:, h * 2 + dc, :], False, dc == 1,
                             [qdTb, Sbfb[h * 2 + dc]], [po.b])
                    k.act(junk.t[:, 0:512], po.t[:], AF.Square, [po.b], [junk.b, rsb], accum_out=rs[:, h:h + 1])
                    k.act(rs[:, 4 + h:5 + h], rs[:, h:h + 1], AF.Sqrt, [rsb], [rsb], scale=1.0 / 512.0, bias=RMS_EPS)
                    k.recip(rs[:, 8 + h:9 + h], rs[:, 4 + h:5 + h], [rsb], [rsb])
                    k.stt("dve", yv[:, h * 512:(h + 1) * 512], po.t[:], rs[:, 8 + h:9 + h], sg[:, h * 512:(h + 1) * 512],
                          ALU.mult, ALU.mult, [po.b, rsb, sgb], [yvb])
                    for dc in range(2):
                        pu = k.ps.next()
                        k.mm(pu.t[:], kd[:, h * 256 + dc * 128:h * 256 + (dc + 1) * 128], v[:, h * 512:(h + 1) * 512], True, True,
                             [kdb, vb], [pu.b])
                        gam = float((1.0 - 2.0 ** (-5.0 - h)) ** 128)
                        i8 = h * 2 + dc
                        k.stt("dve", S[:, i8, :], S[:, i8, :], gam, pu.t[:], ALU.mult, ALU.add, [Sb[i8], pu.b], [Sb[i8]])
                        k.act(Sbf[:, i8, :], S[:, i8, :], AF.Copy, [Sb[i8]], [Sbfb[i8]])
                k.dma("sp", y2_d[t * 128:(t + 1) * 128, :], yv, [yvb], [y2b[t]])

        def outproj(src, base, zlayer=None):
            nz = 4 if zlayer is not None else 0
            off = begin(4 + nz, 3, 128)
            slots, hring, uT = st8["slots"], st8["hring"], st8["uT"]
            if zlayer is not None:
                k.dma("sp", gt.t[:], gbc_d[zlayer, :, :], [], [gt.b])
            load_slots(base, 4 + nz)
            yvs = []
            for i in range(2):
                v_, off = carve(off, [128, 2048], BF16)
                yvs.append(V(v_))
            yvr = Ring(yvs)
            yT, off = carve(off, [128, 16, 128], BF16)
            yTb = Buf()
            szs = []
            for i in range(2):
                v_, off = carve(off, [128, 512], BF16)
                szs.append(V(v_))
            szr = Ring(szs)
            wz = [slots[i].t[:].rearrange("p (kc n) -> p kc n", kc=8) for i in range(nz)]
            wout = [slots[nz + i].t[:].rearrange("p (kc n) -> p kc n", kc=4) for i in range(4)]
            for t in range(NT):
                hb = hring.next()
                yv = yvr.next()
                k.dma("sp", yv.t, y2_d[t * 128:(t + 1) * 128, :], [y2b[t]], [yv.b])
                if zlayer is None:
                    k.dma("sp", hb.t[:], src[t * 128:(t + 1) * 128, :], [hd[t]], [hb.b])
                else:
                    norm_tile(src[t * 128:(t + 1) * 128, :], hd[t], hb, uT.t, uT.b, 0)
                    for zb in range(4):
                        ps = k.ps.next()
                        for kc in range(8):
                            k.mm(ps.t[:], uT.t[:, kc, 0:128], wz[zb][:, kc, :], kc == 0, kc == 7, [slots[zb].b, uT.b], [ps.b])
                        sz = szr.next()
                        k.act(sz.t, ps.t[:], AF.Silu, [ps.b], [sz.b])
                        k.tt("dve", yv.t[:, zb * 512:(zb + 1) * 512], yv.t[:, zb * 512:(zb + 1) * 512], sz.t, ALU.mult,
                             [yv.b, sz.b], [yv.b])
                for half in range(2):
                    ps = k.ps.next()
                    pv = ps.t[:].bitcast(BF16)
                    for c in range(8):
                        col = half * 1024 + c * 128
                        k.tr(pv[:, c * 128:(c + 1) * 128], yv.t[:, col:col + 128], ident_b.t[:], [yv.b, ident_b.b], [ps.b])
                    k.act(yT[:, half * 8:(half + 1) * 8, :], pv.rearrange("p (c t) -> p c t", c=8), AF.Copy, [ps.b], [yTb])
                for ch in range(2):
                    ps = k.ps.next()
                    for ec in range(16):
                        k.mm(ps.t[:], yT[:, ec, :], wout[ec // 4][:, ec % 4, ch * 512:(ch + 1) * 512], ec == 0, ec == 15,
                             [slots[nz + ec // 4].b, yTb], [ps.b])
                    k.tt("dve", hb.t[:, ch * 512:(ch + 1) * 512], hb.t[:, ch * 512:(ch + 1) * 512], ps.t[:], ALU.add,
                         [hb.b, ps.b], [hb.b])
                out_ops.append(finish_tile(hb, t, False))

        def gdn(layer, src, base):
            off = begin(8, 2, 256)
            slots, hring, uT = st8["slots"], st8["hring"], st8["uT"]
            k.dma("sp", gt.t[:], gbc_d[layer, :, :], [], [gt.b])
            load_slots(base, 8)
            cb = Buf()
            wx, off = carve(off, [128, 8, 32], BF16)
            gcw, off = carve(off, [128, 32, 4], F32)
            gvec, off = carve(off, [128, 160], F32)
            gcon, off = carve(off, [128, 5, 128], F32)
            ones, off = carve(off, [128, 128], F32)
            nexpA, off = carve(off, [128, 16], F32)
            halo, off = carve(off, [128, 32, 3], F32)
            halob = [Buf() for _ in range(32)]
            S, off = carve(off, [128, 16, 128], F32)
            Sbf, off = carve(off, [128, 16, 128], BF16)
            Sb = [Buf() for _ in range(8)]
            Sbfb = [Buf() for _ in range(8)]
            cT, off = carve(off, [128, 32, 256], BF16)
            cTb = [Buf() for _ in range(32)]
            raws = []
            for i in range(2):
                v_, off = carve(off, [128, 260], F32)
                raws.append(V(v_))
            rawr = Ring(raws)
            accs = []
            for i in range(2):
                v_, off = carve(off, [128, 256], F32)
                accs.append(V(v_))
            accr = Ring(accs)
            ba, off = carve(off, [128, 32], F32)
            bab = Buf()
            gs, off = carve(off, [128, 12, 16], F32)
            gsb = Buf()
            BETA, SP, GG, GC, EXPG, GL, KDC, GT0, GT1, BG, TMP = range(11)

            def cf(shape, dt):
                nonlocal off
                v_, off = carve(off, shape, dt)
                return V(v_)

            class XS:
                pass

            sets = []
            for si in range(2):
                X = XS()
                X.tm = cf([128, 4, 128], BF16)
                for nm in ("qkn", "qknT", "kb", "kbg", "vbt", "kbT", "kd0", "kd1", "RTb", "wT", "attnT", "vn"):
                    setattr(X, nm, cf([128, 2, 128], BF16))
                for nm in ("sq", "Dg", "dA", "dT", "tf", "RT", "usb", "osb", "sq2"):
                    setattr(X, nm, cf([128, 2, 128], F32))
                X.Pm = [cf([128, 2, 128], F32) for _ in range(2)]
                X.PTm = [cf([128, 2, 128], F32) for _ in range(2)]
                X.l2 = cf([128, 8], F32)
                X.rs = cf([128, 8], F32)
                k.memset("pool", X.kd0.t, 0.0, [X.kd0.b])
                k.memset("pool", X.kd1.t, 0.0, [X.kd1.b])
                sets.append(X)
            yvs = [cf([128, 2048], BF16) for _ in range(1)]
            yvr = Ring(yvs)

            k.dma("pool", wx, gx_d[:, :, :], [], [cb])
            k.dma("sp", gcw, gcw_d[:, :, :], [], [cb])
            k.dma("sp", gvec, gvec_d[:, :], [], [cb])
            k.dma("sp", gcon, gcon_d[:, :, :], [], [cb])
            k.memset("pool", ones, 1.0, [cb])
            k.act(nexpA, gvec[:, 0:16], AF.Exp, [cb], [cb])
            k.ts("dve", nexpA, nexpA, -1.0, None, ALU.mult, None, [cb], [cb])
            k.memset("pool", halo, 0.0, halob)
            k.memset("pool", S, 0.0, Sb)
            k.memset("pool", Sbf, 0.0, Sbfb)
            U_, IND0, IND1, NEGS, NEGT = (gcon[:, i, :] for i in range(5))
            win = [slots[i].t[:].rearrange("p (kc n) -> p kc n", kc=8) for i in range(8)]
            bc4 = lambda ap: ap.unsqueeze(2).to_broadcast([128, 4, 128])
            bch = lambda ap: ap.unsqueeze(1).to_broadcast([128, 4, 128])

            for blk in range(T // 256):
                for i in range(2):
                    t = blk * 2 + i
                    hb = hring.next()
                    norm_tile(src[t * 128:(t + 1) * 128, :], hd[t], hb, uT.t, uT.b, i * 128)
                for ch2 in range(16):
                    ps = k.ps.next()
                    for half in range(2):
                        ch = ch2 * 2 + half
                        for kc in range(8):
                            k.mm(ps.t[:, half * 256:(half + 1) * 256], win[ch // 4][:, kc, (ch % 4) * 128:(ch % 4 + 1) * 128],
                                 uT.t[:, kc, :], kc == 0, kc == 7, [slots[ch // 4].b, uT.b], [ps.b])
                    for half in range(2):
                        ch = ch2 * 2 + half
                        rw = rawr.next()
                        k.cp("pool", rw.t[:, 0:3], halo[:, ch, :], [halob[ch]], [rw.b])
                        k.act(rw.t[:, 3:259], ps.t[:, half * 256:(half + 1) * 256], AF.Copy, [ps.b], [rw.b])
                        ac = accr.next()
                        k.ts("dve", ac.t, rw.t[:, 0:256], gcw[:, ch, 0:1], None, ALU.mult, None, [rw.b, cb], [ac.b])
                        for kk in range(1, 4):
                            k.stt("dve", ac.t, rw.t[:, kk:kk + 256], gcw[:, ch, kk:kk + 1], ac.t, ALU.mult, ALU.add,
                                  [rw.b, cb, ac.b], [ac.b])
                        k.cp("pool", halo[:, ch, :], rw.t[:, 256:259], [rw.b], [halob[ch]])
                        k.act(cT[:, ch, :], ac.t, AF.Silu, [ac.b], [cTb[ch]])
                for i in range(2):
                    t = blk * 2 + i
                    tok = slice(i * 128, (i + 1) * 128)
                    ps = k.ps.next()
                    for kc in range(8):
                        k.mm(ps.t[:, 0:32], uT.t[:, kc, tok], wx[:, kc, :], kc == 0, kc == 7, [cb, uT.b], [ps.b])
                    k.act(ba, ps.t[:, 0:32], AF.Copy, [ps.b], [bab])
                    g_ = lambda j: gs[:, j, :]
                    k.act(g_(BETA), ba[:, 0:16], AF.Sigmoid, [bab], [gsb])
                    k.tt("dve", g_(TMP), ba[:, 16:32], gvec[:, 16:32], ALU.add, [bab, cb], [gsb])
                    k.act(g_(TMP), g_(TMP), AF.Exp, [gsb], [gsb])
                    k.act(g_(SP), g_(TMP), AF.Ln, [gsb], [gsb], bias=1.0)
                    k.tt("dve", g_(GG), g_(SP), nexpA, ALU.mult, [gsb, cb], [gsb])
                    ps = k.ps.next()
                    k.mm(ps.t[:, 0:16], U_, g_(GG), True, True, [cb, gsb], [ps.b])
                    k.mm(ps.t[:, 16:32], IND0, g_(GG), True, True, [cb, gsb], [ps.b])
                    k.mm(ps.t[:, 32:48], IND1, g_(GG), True, True, [cb, gsb], [ps.b])
                    k.act(g_(GC), ps.t[:, 0:16], AF.Copy, [ps.b], [gsb])
                    k.act(g_(EXPG), ps.t[:, 0:16], AF.Exp, [ps.b], [gsb])
                    k.act(g_(GT0), ps.t[:, 16:32], AF.Exp, [ps.b], [gsb])
                    k.act(g_(GT1), ps.t[:, 32:48], AF.Exp, [ps.b], [gsb])
                    k.act(gs[0:64, GL, :], ps.t[0:64, 16:32], AF.Copy, [ps.b], [gsb])
                    k.act(gs[64:128, GL, :], ps.t[64:128, 32:48], AF.Copy, [ps.b], [gsb])
                    k.tt("dve", g_(KDC), g_(GL), g_(GC), ALU.subtract, [gsb], [gsb])
                    k.act(g_(KDC), g_(KDC), AF.Exp, [gsb], [gsb])
                    k.tt("dve", g_(BG), g_(BETA), g_(EXPG), ALU.mult, [gsb], [gsb])
                    yv = yvr.next()
                    def hp(hq, X, tok=tok, yv=yv):
                        hs = slice(2 * hq, 2 * hq + 2)
                        sc2 = lambda ap: ap.unsqueeze(2).to_broadcast([128, 2, 128])
                        b2 = lambda ap: ap.unsqueeze(1).to_broadcast([128, 2, 128])
                        r2 = lambda ap: ap.rearrange("p (h c) -> p h c", h=2)
                        chs = [hq, 8 + hq, 16 + 2 * hq, 17 + 2 * hq]
                        ps = k.ps.next()
                        pv = ps.t[:].bitcast(BF16)
                        for j, ch in enumerate(chs):
                            k.tr(pv[:, j * 128:(j + 1) * 128], cT[:, ch, tok], ident_b.t[:], [cTb[ch], ident_b.b], [ps.b])
                        k.act(X.tm.t, pv[:, 0:512].rearrange("p (c t) -> p c t", c=4), AF.Copy, [ps.b], [X.tm.b])
                        yield
                        k.tt("dve", X.sq.t, X.tm.t[:, 0:2, :], X.tm.t[:, 0:2, :], ALU.mult, [X.tm.b], [X.sq.b])
                        P.op("dve", lambda e: e.tensor_reduce(out=X.l2.t[:, 0:2], in_=X.sq.t, axis=AX.X, op=ALU.add), [X.sq.b], [X.l2.b])
                        k.act(X.l2.t[:, 2:4], X.l2.t[:, 0:2], AF.Sqrt, [X.l2.b], [X.l2.b], bias=1e-6)
                        yield
                        k.recip(X.l2.t[:, 4:6], X.l2.t[:, 2:4], [X.l2.b], [X.l2.b])
                        k.ts("dve", X.l2.t[:, 4:5], X.l2.t[:, 4:5], 128.0 ** -0.5, None, ALU.mult, None, [X.l2.b], [X.l2.b])
                        k.tt("dve", X.qkn.t, X.tm.t[:, 0:2, :], sc2(X.l2.t[:, 4:6]), ALU.mult, [X.tm.b, X.l2.b], [X.qkn.b])
                        ps = k.ps.next()
                        pv = ps.t[:].bitcast(BF16)
                        for j in range(2):
                            k.tr(pv[:, j * 128:(j + 1) * 128], X.qkn.t[:, j, :], ident_b.t[:], [X.qkn.b, ident_b.b], [ps.b])
                        k.act(X.qknT.t, pv[:, 0:256].rearrange("p (c t) -> p c t", c=2), AF.Copy, [ps.b], [X.qknT.b])
                        kn2 = X.qkn.t[:, 1:2, :].to_broadcast([128, 2, 128])
                        k.tt("dve", X.kb.t, kn2, sc2(gs[:, BETA, hs]), ALU.mult, [X.qkn.b, gsb], [X.kb.b])
                        k.tt("pool", X.kbg.t, kn2, sc2(gs[:, BG, hs]), ALU.mult, [X.qkn.b, gsb], [X.kbg.b])
                        k.tt("pool", X.kd0.t[0:64], kn2[0:64], sc2(gs[:, KDC, hs])[0:64], ALU.mult, [X.qkn.b, gsb], [X.kd0.b])
                        k.tt("pool", X.kd1.t[64:128], kn2[64:128], sc2(gs[:, KDC, hs])[64:128], ALU.mult, [X.qkn.b, gsb], [X.kd1.b])
                        k.tt("pool", X.vbt.t, X.tm.t[:, 2:4, :], sc2(gs[:, BETA, hs]), ALU.mult, [X.tm.b, gsb], [X.vbt.b])
                        yield
                        ps = k.ps.next()
                        pv = ps.t[:].bitcast(BF16)
                        for j in range(2):
                            k.tr(pv[:, j * 128:(j + 1) * 128], X.kb.t[:, j, :], ident_b.t[:], [X.kb.b, ident_b.b], [ps.b])
                        k.act(X.kbT.t, pv[:, 0:256].rearrange("p (c t) -> p c t", c=2), AF.Copy, [ps.b], [X.kbT.b])
                        k.tt("dve", X.Dg.t, b2(ident_f.t[:]), sc2(gs[:, GC, hs]), ALU.mult, [ident_f.b, gsb], [X.Dg.b])
                        yield
                        pg = k.ps.next()
                        k.mm(pg.t[:, 0:256], ones, X.Dg.t.rearrange("p h c -> p (h c)"), True, True, [cb, X.Dg.b], [pg.b])
                        pg3 = r2(pg.t[:, 0:256])
                        k.stt("dve", X.tf.t, pg3, -1.0, sc2(gs[:, GC, hs]), ALU.mult, ALU.add, [pg.b, gsb], [X.tf.b])
                        k.tt("dve", X.dT.t, pg3, sc2(gs[:, GC, hs]), ALU.subtract, [pg.b, gsb], [X.dT.b])
                        k.tt("pool", X.tf.t, X.tf.t, b2(NEGS), ALU.add, [X.tf.b, cb], [X.tf.b])
                        k.tt("pool", X.dT.t, X.dT.t, b2(NEGT), ALU.add, [X.dT.b, cb], [X.dT.b])
                        yield
                        k.act(X.dA.t, X.tf.t, AF.Exp, [X.tf.b], [X.dA.b])
                        k.act(X.dT.t, X.dT.t, AF.Exp, [X.dT.b], [X.dT.b])
                        ps = k.ps.next()
                        for j in range(2):
                            k.mm(ps.t[:, j * 128:(j + 1) * 128], X.kbT.t[:, j, :], X.qknT.t[:, 1, :], True, True,
                                 [X.kbT.b, X.qknT.b], [ps.b])
                        cur = 0
                        k.stt("dve", X.Pm[cur].t, r2(ps.t[:, 0:256]), -1.0, X.dA.t, ALU.mult, ALU.mult, [ps.b, X.dA.b], [X.Pm[cur].b])
                        yield
                        ps = k.ps.next()
                        for j in range(2):
                            k.tr(ps.t[:, j * 128:(j + 1) * 128], X.Pm[cur].t[:, j, :], ident_f.t[:], [X.Pm[cur].b, ident_f.b], [ps.b])
                        k.act(X.PTm[cur].t, r2(ps.t[:, 0:256]), AF.Copy, [ps.b], [X.PTm[cur].b])
                        k.tt("dve", X.RT.t, X.PTm[cur].t, b2(ident_f.t[:]), ALU.add, [X.PTm[cur].b, ident_f.b], [X.RT.b])
                        ps = k.ps.next()
                        k.mm(ps.t[:, 0:128], X.qknT.t[:, 1, :], X.qknT.t[:, 0, :], True, True, [X.qknT.b], [ps.b])
                        k.tt("dve", X.attnT.t, ps.t[:, 0:128].unsqueeze(1).to_broadcast([128, 2, 128]), X.dT.t, ALU.mult,
                             [ps.b, X.dT.b], [X.attnT.b])
                        yield
                        for lvl in range(1, 6):
                            nxt = 1 - cur
                            ps = k.ps.next()
                            for j in range(2):
                                k.mm(ps.t[:, j * 128:(j + 1) * 128], X.PTm[cur].t[:, j, :], X.Pm[cur].t[:, j, :], True, True,
                                     [X.PTm[cur].b, X.Pm[cur].b], [ps.b])
                            k.act(X.Pm[nxt].t, r2(ps.t[:, 0:256]), AF.Copy, [ps.b], [X.Pm[nxt].b])
                            if lvl < 5:
                                ps = k.ps.next()
                                for j in range(2):
                                    k.mm(ps.t[:, j * 128:(j + 1) * 128], X.Pm[cur].t[:, j, :], X.PTm[cur].t[:, j, :], True, True,
                                         [X.PTm[cur].b, X.Pm[cur].b], [ps.b])
                                k.cp("dve", X.PTm[nxt].t, r2(ps.t[:, 0:256]), [ps.b], [X.PTm[nxt].b])
                            yield
                            ps = k.ps.next()
                            for j in range(2):
                                k.mm(ps.t[:, j * 128:(j + 1) * 128], X.Pm[nxt].t[:, j, :], X.RT.t[:, j, :], True, True,
                                     [X.Pm[nxt].b, X.RT.b], [ps.b])
                            k.tt("dve", X.RT.t, X.RT.t, r2(ps.t[:, 0:256]), ALU.add, [X.RT.b, ps.b], [X.RT.b])
                            cur = nxt
                            yield
                        k.act(X.RTb.t, X.RT.t, AF.Copy, [X.RT.b], [X.RTb.b])
                        yield
                        ps = k.ps.next()
                        for j in range(2):
                            k.mm(ps.t[:, j * 128:(j + 1) * 128], X.RTb.t[:, j, :], X.vbt.t[:, j, :], True, True, [X.RTb.b, X.vbt.b], [ps.b])
                        k.act(X.usb.t, r2(ps.t[:, 0:256]), AF.Copy, [ps.b], [X.usb.b])
                        ps = k.ps.next()
                        for j in range(2):
                            k.mm(ps.t[:, j * 128:(j + 1) * 128], X.kbg.t[:, j, :], X.RTb.t[:, j, :], True, True, [X.RTb.b, X.kbg.b], [ps.b])
                        k.act(X.wT.t, r2(ps.t[:, 0:256]), AF.Copy, [ps.b], [X.wT.b])
                        yield
                        S2 = S[:, hs, :]
                        pw = k.ps.next()
                        for j in range(2):
                            k.mm(pw.t[:, j * 128:(j + 1) * 128], X.wT.t[:, j, :], Sbf[:, 2 * hq + j, :], True, True,
                                 [X.wT.b, Sbfb[hq]], [pw.b])
                        k.tt("dve", X.vn.t, X.usb.t, r2(pw.t[:, 0:256]), ALU.subtract, [X.usb.b, pw.b], [X.vn.b])
                        pq = k.ps.next()
                        for j in range(2):
                            k.mm(pq.t[:, j * 128:(j + 1) * 128], X.qknT.t[:, 0, :], Sbf[:, 2 * hq + j, :], True, True,
                                 [X.qknT.b, Sbfb[hq]], [pq.b])
                        k.tt("dve", X.osb.t[0:64], r2(pq.t[0:64, 0:256]), sc2(gs[:, EXPG, hs])[0:64], ALU.mult,
                             [pq.b, gsb], [X.osb.b])
                        yield
                        for cchunk, (kdx, gtx) in enumerate(((X.kd0, GT0), (X.kd1, GT1))):
                            pu = k.ps.next()
                            for j in range(2):
                                k.mm(pu.t[:, j * 128:(j + 1) * 128], kdx.t[:, j, :], X.vn.t[:, j, :], True, True, [kdx.b, X.vn.b], [pu.b])
                            k.tt("dve", S2, S2, sc2(gs[:, gtx, hs]), ALU.mult, [Sb[hq], gsb], [Sb[hq]])
                            k.tt("dve", S2, S2, r2(pu.t[:, 0:256]), ALU.add, [Sb[hq], pu.b], [Sb[hq]])
                            k.act(Sbf[:, hs, :], S2, AF.Copy, [Sb[hq]], [Sbfb[hq]])
                            yield
                            if cchunk == 0:
                                pw = k.ps.next()
                                for j in range(2):
                                    k.mm(pw.t[:, j * 128:(j + 1) * 128], X.wT.t[:, j, :], Sbf[:, 2 * hq + j, :], True, True,
                                         [X.wT.b, Sbfb[hq]], [pw.b])
                                k.tt("dve", X.vn.t[64:128], X.usb.t[64:128], r2(pw.t[64:128, 0:256]), ALU.subtract,
                                     [X.usb.b, pw.b], [X.vn.b])
                                pq1 = k.ps.next()
                                for j in range(2):
                                    k.mm(pq1.t[:, j * 128:(j + 1) * 128], X.qknT.t[:, 0, :], Sbf[:, 2 * hq + j, :], True, True,
                                         [X.qknT.b, Sbfb[hq]], [pq1.b])
                                k.tt("dve", X.osb.t[64:128], r2(pq1.t[64:128, 0:256]), sc2(gs[:, EXPG, hs])[64:128], ALU.mult,
                                     [pq1.b, gsb], [X.osb.b])
                                yield
                        pa = k.ps.next()
                        for j in range(2):
                            k.mm(pa.t[:, j * 128:(j + 1) * 128], X.attnT.t[:, j, :], X.vn.t[:, j, :], True, True, [X.attnT.b, X.vn.b], [pa.b])
                        k.tt("dve", X.osb.t, X.osb.t, r2(pa.t[:, 0:256]), ALU.add, [X.osb.b, pa.b], [X.osb.b])
                        yield
                        k.tt("pool", X.sq2.t, X.osb.t, X.osb.t, ALU.mult, [X.osb.b], [X.sq2.b])
                        P.op("dve", lambda e: e.tensor_reduce(out=X.rs.t[:, 0:2], in_=X.sq2.t, axis=AX.X, op=ALU.add), [X.sq2.b], [X.rs.b])
                        k.act(X.rs.t[:, 2:4], X.rs.t[:, 0:2], AF.Sqrt, [X.rs.b], [X.rs.b], scale=1.0 / 128.0, bias=RMS_EPS)
                        yield
                        k.recip(X.rs.t[:, 4:6], X.rs.t[:, 2:4], [X.rs.b], [X.rs.b])
                        k.tt("dve", X.osb.t, X.osb.t, sc2(X.rs.t[:, 4:6]), ALU.mult, [X.osb.b, X.rs.b], [X.osb.b])
                        k.tt("pool", r2(yv.t[:, hq * 256:(hq + 1) * 256]), X.osb.t, b2(gvec[:, 32:160]), ALU.mult, [X.osb.b, cb], [yv.b])

                    for pair in range(4):
                        gens = [hp(2 * pair, sets[0]), hp(2 * pair + 1, sets[1])]
                        alive = [True, True]
                        while any(alive):
                            for gi in range(2):
                                if alive[gi]:
                                    try:
                                        next(gens[gi])
                                    except StopIteration:
                                        alive[gi] = False
                    k.dma("sp", y2_d[t * 128:(t + 1) * 128, :], yv.t, [yv.b], [y2b[t]])

        base = 0
        sub = 0
        cnt = {0: 0, 1: 0, 2: 0}
        for layer in range(4):
            kind = KINDS[layer]
            src = x_d if layer == 0 else h_d
            if sub < n_sub:
                if kind == 0:
                    lru(cnt[0], layer, src, base)
                elif kind == 1:
                    ret(layer, src, base)
                    outproj(src, base + 12)
                else:
                    gdn(layer, src, base)
                    outproj(src, base + 8, zlayer=layer)
            cnt[kind] += 1
            base += NSLOT[kind]
            sub += 1
            if sub < n_sub:
                mlp(layer, h_d, base, layer == 3 and final)
            base += 16
            sub += 1
        P.barrier()
        fin = fin_ops
        if not fin:
            srcf = h_d if n_sub > 0 else x_d
            off_ = begin(0, 4, 128)
            load_gt2(off_)
            hring = st8["hring"]
            for t in range(NT):
                hb = hring.next()
                k.dma("sp", hb.t[:], srcf[t * 128:(t + 1) * 128, :], [hd[t]], [hb.b])
                if final:
                    fin.append(finish_tile(hb, t, True))
                else:
                    fin.append(k.dma("sp", y_d[t * 128:(t + 1) * 128, :], hb.t[:], [hb.b], [hd[t]]))
        P.finalize_and_emit(fin)
    return nc


def _colblk(w, c0):
    return np.ascontiguousarray(w[:, c0:c0 + 512].reshape(8, 128, 512).transpose(1, 0, 2)).reshape(128, 4096)


def _rowblk(w, r0):
    return np.ascontiguousarray(w[r0:r0 + 512, :].reshape(4, 128, 1024).transpose(1, 0, 2)).reshape(128, 4096)


def _chan(v):
    return v.reshape(8, 128).T


def pack_inputs(inp, T):
    f = lambda a: np.asarray(a, dtype=np.float32)
    slots = []
    cnt = {0: 0, 1: 0, 2: 0}
    for layer in range(4):
        kind = KINDS[layer]
        j = cnt[kind]
        cnt[kind] += 1
        if kind == 0:
            w_in, w_out = f(inp["lru_w_in"][j]), f(inp["lru_w_out"][j])
            for i in range(4):
                slots.append(_colblk(w_in, i * 512))
            for i in range(2):
                slots.append(_rowblk(w_out, i * 512))
            wa, wx = f(inp["lru_wa"][j]), f(inp["lru_wx"][j])
            g = lambda w: w.reshape(4, 2, 128, 256).transpose(2, 0, 1, 3).reshape(128, 2048)
            slots.append(np.concatenate([g(wa), g(wx)], axis=1))
        elif kind == 1:
            w_in, w_out = f(inp["ret_w_in"][j]), f(inp["ret_w_out"][j])
            for i in range(12):
                slots.append(_colblk(w_in, i * 512))
            for i in range(4):
                slots.append(_rowblk(w_out, i * 512))
        else:
            w_in, w_out = f(inp["gdn_w_in"][j]), f(inp["gdn_w_out"][j])
            for i in range(12):
                slots.append(_colblk(w_in, i * 512))
            for i in range(4):
                slots.append(_rowblk(w_out, i * 512))
        wu, wd = f(inp["mlp_w_up"][layer]), f(inp["mlp_w_down"][layer])
        for i in range(8):
            slots.append(_colblk(wu, i * 512))
        for i in range(8):
            slots.append(_rowblk(wd, i * 512))
    wslots = np.stack(slots, 0)
    gs = [f(inp["mix_norm"][i]) for i in range(4)] + [f(inp["mlp_norm"][i]) for i in range(4)] + [f(inp["final_norm"])]
    gbc = np.stack([np.broadcast_to(g[None, :], (128, D)) for g in gs], 0).copy()
    lrup = np.zeros((2, 128, 8, 8), np.float32)
    for j in range(2):
        cw = f(inp["lru_conv_w"][j])
        for kk in range(4):
            lrup[j, :, :, kk] = _chan(cw[kk])
        lrup[j, :, :, 4] = _chan(f(inp["lru_conv_b"][j]))
        lrup[j, :, :, 5] = _chan(f(inp["lru_ba"][j]).reshape(-1))
        lrup[j, :, :, 6] = _chan(f(inp["lru_bx"][j]).reshape(-1))
        lrup[j, :, :, 7] = _chan(f(inp["lru_lambda"][j]))
    pos = np.arange(T, dtype=np.float32)
    inv_freq = (np.float32(10000.0) ** (-np.arange(0, 256, 2, dtype=np.float32) / np.float32(256))).astype(np.float32)
    ang = (pos[:, None] * inv_freq[None, :]).astype(np.float32)
    rcos, rsin = np.cos(ang).astype(np.float32), np.sin(ang).astype(np.float32)
    lg = np.log1p(-np.exp2(-5.0 - np.arange(4, dtype=np.float64)))
    ii = np.arange(128)
    diff = ii[None, :] - ii[:, None]
    rmask = np.stack([np.where(diff >= 0, np.exp(lg[h] * np.maximum(diff, 0)), 0.0) for h in range(4)], 1).astype(np.float32)
    rdec = np.stack([np.broadcast_to(np.exp(lg[h] * (ii + 1.0))[None, :], (128, 128)) for h in range(4)], 1).astype(np.float32)
    rkdec = np.stack([np.exp(lg[h] * (127.0 - ii)) / 16.0 for h in range(4)], 1).astype(np.float32)
    gw = f(inp["gdn_w_in"][0])
    gdn_wx = np.ascontiguousarray(gw[:, 6144:6176].reshape(8, 128, 32).transpose(1, 0, 2))
    gcw = f(inp["gdn_conv_w"][0])
    gdn_cw = np.ascontiguousarray(gcw.reshape(4, 32, 128).transpose(2, 1, 0))
    gdn_vec = np.zeros((128, 160), np.float32)
    gdn_vec[:, 0:16] = f(inp["gdn_a_log"][0])[None, :]
    gdn_vec[:, 16:32] = f(inp["gdn_dt_bias"][0])[None, :]
    gdn_vec[:, 32:160] = f(inp["gdn_norm"][0])[None, :]
    same = (ii[:, None] // 64) == (ii[None, :] // 64)
    gcon = np.zeros((128, 5, 128), np.float32)
    gcon[:, 0, :] = (same & (ii[:, None] <= ii[None, :]))
    gcon[:, 1, :] = np.broadcast_to((ii < 64)[:, None], (128, 128))
    gcon[:, 2, :] = np.broadcast_to((ii >= 64)[:, None], (128, 128))
    gcon[:, 3, :] = np.where(same & (ii[:, None] > ii[None, :]), 0.0, -30000.0)
    gcon[:, 4, :] = np.where(same & (ii[:, None] <= ii[None, :]), 0.0, -30000.0)
    shared = dict(wslots=wslots, gbc=gbc, lrup=lrup, ident=np.eye(128, dtype=np.float32), rcos=rcos, rsin=rsin,
                  rmask=rmask, rdec=rdec, rkdec=rkdec, gdn_wx=gdn_wx, gdn_cw=gdn_cw, gdn_vec=gdn_vec, gdn_con=gcon)
    return shared


_NC_CACHE = {}


def kernel(**inputs):
    x = np.asarray(inputs["x"], dtype=np.float32)
    B, T, _ = x.shape
    shared = pack_inputs(inputs, T)
    key = (T, 8, True)
    if key not in _NC_CACHE:
        _NC_CACHE[key] = build_nc(T)
    nc = _NC_CACHE[key]
    in_maps = []
    for c in range(NCORES):
        m = dict(shared)
        m["x"] = np.ascontiguousarray(x[c % B])
        in_maps.append(m)
    res = run_bass_kernel_spmd(nc, in_maps, core_ids=list(range(NCORES)))
    return np.stack([np.asarray(res.results[b]["y"], dtype=np.float32) for b in range(B)], 0)
```
